# Optimizing a Trainium2 kernel written in Bass

```python
import math
import jax, jax.numpy as jnp
from jax import lax
import numpy as np

D_MODEL = 1024
BATCH = 2
SEQ = 8192
DEPTH = 2

N_A_LAYERS = DEPTH // 2
N_B_LAYERS = DEPTH - N_A_LAYERS
LRU_WIDTH = 1280
LRU_BLOCK = 256
LRU_HEADS = LRU_WIDTH // LRU_BLOCK
CONV_WIDTH = 4
LRU_C = 8.0
N_HEADS = 8
N_KV_HEADS = 4
HEAD_DIM = 128
KV_GROUP = N_HEADS // N_KV_HEADS
MOBA_BLOCK = 256
MOBA_TOPK = 3
Q_CHUNK = 16
REL_BUCKETS = 32
REL_MAX_DIST = 128
D_FF = -(-8 * D_MODEL // (3 * 256)) * 256
RMS_EPS = 1e-6
NEG_INF = -1e30

kernel_name = "yoco_rglru_moba_adaln_trunk"


def rmsnorm(x, g):
    xf = x.astype(jnp.float32)
    y = xf * lax.rsqrt(jnp.mean(xf * xf, axis=-1, keepdims=True) + RMS_EPS)
    return (y * g.astype(jnp.float32)).astype(x.dtype)


def modulate(h, shift, scale):
    return h * (1 + scale[:, None, :]) + shift[:, None, :]


def swiglu(h, w_gate, w_up, w_down):
    return (jax.nn.silu(h @ w_gate) * (h @ w_up)) @ w_down


def _lru_combine(left, right):
    a_l, b_l = left
    a_r, b_r = right
    return a_l * a_r, a_r * b_l + b_r


def rglru_mixer(h, w_in, conv_w, conv_b, w_gates, b_gates, lru_lambda, w_out):
    B, S, _ = h.shape
    u = h @ w_in
    y_br = jax.nn.gelu(u[..., :LRU_WIDTH])
    x_br = u[..., LRU_WIDTH:]
    xp = jnp.pad(x_br, ((0, 0), (CONV_WIDTH - 1, 0), (0, 0)))
    xc = sum(xp[:, k:k + S, :] * conv_w[k] for k in range(CONV_WIDTH)) + conv_b
    xb = xc.reshape(B, S, LRU_HEADS, LRU_BLOCK)
    g = jnp.einsum('bshi,hio->bsho', xb, w_gates)
    gr = g[..., :LRU_BLOCK].reshape(B, S, LRU_WIDTH) + b_gates[0]
    gi = g[..., LRU_BLOCK:].reshape(B, S, LRU_WIDTH) + b_gates[1]
    r = jax.nn.sigmoid(gr.astype(jnp.float32))
    i = jax.nn.sigmoid(gi.astype(jnp.float32))
    log_a = -LRU_C * r * jax.nn.softplus(-lru_lambda.astype(jnp.float32))
    a = jnp.exp(log_a)
    mult = jnp.sqrt(-jnp.expm1(2.0 * log_a))
    b = mult * (i * xc.astype(jnp.float32))
    _, hs = lax.associative_scan(_lru_combine, (a, b), axis=1)
    return (hs.astype(h.dtype) * y_br) @ w_out


def shared_kv(x, shift, scale, g, w_kv):
    B, S, _ = x.shape
    h = modulate(rmsnorm(x, g), shift, scale)
    kv = h @ w_kv
    nb = -(-S // MOBA_BLOCK)
    pad = nb * MOBA_BLOCK - S
    def blk(t):
        t = t.reshape(B, S, N_KV_HEADS, HEAD_DIM).transpose(0, 2, 1, 3)
        t = jnp.pad(t, ((0, 0), (0, 0), (0, pad), (0, 0)))
        return t.reshape(B, N_KV_HEADS, nb, MOBA_BLOCK, HEAD_DIM)
    k_blocks = blk(kv[..., :N_KV_HEADS * HEAD_DIM])
    v_blocks = blk(kv[..., N_KV_HEADS * HEAD_DIM:])
    k_mean = jnp.mean(k_blocks.astype(jnp.float32), axis=3)
    return k_blocks, v_blocks, k_mean


def t5_bucket(dist):
    dist = jnp.maximum(dist, 0)
    max_exact = REL_BUCKETS // 2
    d = jnp.maximum(dist, 1).astype(jnp.float32)
    large = max_exact + (jnp.log(d / max_exact) / math.log(REL_MAX_DIST / max_exact)
                         * (REL_BUCKETS - max_exact)).astype(jnp.int32)
    large = jnp.minimum(large, REL_BUCKETS - 1)
    return jnp.where(dist < max_exact, dist, large)


def moba_mixer(h, w_q, w_o, k_blocks, v_blocks, k_mean, rel_bias):
    B, S, _ = h.shape
    nb = k_blocks.shape[2]
    s_pad = nb * MOBA_BLOCK
    n_sel = min(MOBA_TOPK, nb)
    scale = HEAD_DIM ** -0.5
    q = (h @ w_q).reshape(B, S, N_HEADS, HEAD_DIM).transpose(0, 2, 1, 3)
    q = jnp.pad(q, ((0, 0), (0, 0), (0, s_pad - S), (0, 0)))
    kv_head = jnp.arange(N_HEADS) // KV_GROUP
    km = k_mean[:, kv_head]
    bi = jnp.arange(B)[:, None, None, None]
    kvh = kv_head[None, :, None, None]
    hi5 = jnp.arange(N_HEADS)[None, :, None, None, None]
    bias_tab = rel_bias.astype(jnp.float32)
    blk_pos = jnp.arange(MOBA_BLOCK)

    def chunk(ci):
        t0 = ci * Q_CHUNK
        qf = lax.dynamic_slice_in_dim(q, t0, Q_CHUNK, axis=2).astype(jnp.float32)
        pos = t0 + jnp.arange(Q_CHUNK)
        own = t0 // MOBA_BLOCK
        gate = jnp.einsum('bhcd,bhnd->bhcn', qf, km)
        gate = jnp.where(jnp.arange(nb) < own, gate, NEG_INF)
        _, idx = lax.top_k(gate, n_sel)
        sel_valid = jnp.arange(n_sel) < own
        ksel = k_blocks[bi, kvh, idx].astype(jnp.float32)
        vsel = v_blocks[bi, kvh, idx].astype(jnp.float32)
        s_sel = jnp.einsum('bhcd,bhcnkd->bhcnk', qf, ksel) * scale
        kpos_sel = idx[..., None] * MOBA_BLOCK + blk_pos
        dist_sel = pos[None, None, :, None, None] - kpos_sel
        s_sel = jnp.where(sel_valid[:, None], s_sel + bias_tab[hi5, t5_bucket(dist_sel)], NEG_INF)
        k_own = lax.dynamic_index_in_dim(k_blocks, own, axis=2, keepdims=False)[:, kv_head].astype(jnp.float32)
        v_own = lax.dynamic_index_in_dim(v_blocks, own, axis=2, keepdims=False)[:, kv_head].astype(jnp.float32)
        s_own = jnp.einsum('bhcd,bhkd->bhck', qf, k_own) * scale
        dist_own = pos[:, None] - (own * MOBA_BLOCK + blk_pos)[None, :]
        s_own = jnp.where(dist_own >= 0, s_own + bias_tab[:, t5_bucket(dist_own)], NEG_INF)
        logits = jnp.concatenate([s_sel.reshape(B, N_HEADS, Q_CHUNK, n_sel * MOBA_BLOCK), s_own], axis=-1)
        p = jax.nn.softmax(logits, axis=-1)
        p_sel = p[..., :n_sel * MOBA_BLOCK].reshape(B, N_HEADS, Q_CHUNK, n_sel, MOBA_BLOCK)
        p_own = p[..., n_sel * MOBA_BLOCK:]
        o = jnp.einsum('bhcnk,bhcnkd->bhcd', p_sel, vsel) + jnp.einsum('bhck,bhkd->bhcd', p_own, v_own)
        return o.astype(h.dtype)

    o = lax.map(chunk, jnp.arange(s_pad // Q_CHUNK))
    o = o.transpose(1, 0, 3, 2, 4).reshape(B, s_pad, N_HEADS * HEAD_DIM)[:, :S]
    return o @ w_o


def setup_inputs(seed: int = 0) -> dict:
    key = jax.random.key(seed)
    ks = jax.random.split(key, 32)
    nrm = lambda k, shape, s: jax.random.normal(k, shape, jnp.float32) * s
    D = D_MODEL
    u = jax.random.uniform(ks[10], (N_A_LAYERS, LRU_WIDTH), jnp.float32, 0.9, 0.999)
    a_base = u ** (1.0 / LRU_C)
    lru_lambda = jnp.log(a_base) - jnp.log1p(-a_base)
    return {
        "x": nrm(ks[0], (BATCH, SEQ, D), 1.0),
        "c": nrm(ks[1], (BATCH, D), 1.0),
        "mod_w": nrm(ks[2], (DEPTH, D, 6 * D), 0.5 * D ** -0.5),
        "mod_b": nrm(ks[3], (DEPTH, 6 * D), 0.02),
        "norm_mix": 1.0 + nrm(ks[4], (DEPTH, D), 0.02),
        "norm_ffn": 1.0 + nrm(ks[5], (DEPTH, D), 0.02),
        "lru_w_in": nrm(ks[6], (N_A_LAYERS, D, 2 * LRU_WIDTH), D ** -0.5),
        "lru_conv_w": nrm(ks[7], (N_A_LAYERS, CONV_WIDTH, LRU_WIDTH), CONV_WIDTH ** -0.5),
        "lru_conv_b": nrm(ks[8], (N_A_LAYERS, LRU_WIDTH), 0.02),
        "lru_w_gates": nrm(ks[9], (N_A_LAYERS, LRU_HEADS, LRU_BLOCK, 2 * LRU_BLOCK), LRU_BLOCK ** -0.5),
        "lru_b_gates": nrm(ks[11], (N_A_LAYERS, 2, LRU_WIDTH), 0.02),
        "lru_lambda": lru_lambda,
        "lru_w_out": nrm(ks[12], (N_A_LAYERS, LRU_WIDTH, D), LRU_WIDTH ** -0.5),
        "kv_mod_w": nrm(ks[13], (D, 2 * D), 0.5 * D ** -0.5),
        "kv_mod_b": nrm(ks[14], (2 * D,), 0.02),
        "kv_norm": 1.0 + nrm(ks[15], (D,), 0.02),
        "w_kv": nrm(ks[16], (D, 2 * N_KV_HEADS * HEAD_DIM), D ** -0.5),
        "attn_w_q": nrm(ks[17], (N_B_LAYERS, D, N_HEADS * HEAD_DIM), D ** -0.5),
        "attn_w_o": nrm(ks[18], (N_B_LAYERS, N_HEADS * HEAD_DIM, D), (N_HEADS * HEAD_DIM) ** -0.5),
        "rel_bias": nrm(ks[19], (N_HEADS, REL_BUCKETS), 0.2),
        "ffn_w_gate": nrm(ks[20], (DEPTH, D, D_FF), D ** -0.5),
        "ffn_w_up": nrm(ks[21], (DEPTH, D, D_FF), D ** -0.5),
        "ffn_w_down": nrm(ks[22], (DEPTH, D_FF, D), D_FF ** -0.5),
        "final_norm": 1.0 + nrm(ks[23], (D,), 0.02),
    }


def reference(x, c, mod_w, mod_b, norm_mix, norm_ffn, lru_w_in, lru_conv_w, lru_conv_b,
              lru_w_gates, lru_b_gates, lru_lambda, lru_w_out, kv_mod_w, kv_mod_b, kv_norm,
              w_kv, attn_w_q, attn_w_o, rel_bias, ffn_w_gate, ffn_w_up, ffn_w_down, final_norm):
    D = D_MODEL
    cs = jax.nn.silu(c)
    kv = None
    for l in range(DEPTH):
        if l == N_A_LAYERS:
            kv_mod = cs @ kv_mod_w + kv_mod_b
            kv = shared_kv(x, kv_mod[:, :D], kv_mod[:, D:], kv_norm, w_kv)
        mod = cs @ mod_w[l] + mod_b[l]
        sh_m, sc_m, g_m = mod[:, :D], mod[:, D:2 * D], mod[:, 2 * D:3 * D]
        sh_f, sc_f, g_f = mod[:, 3 * D:4 * D], mod[:, 4 * D:5 * D], mod[:, 5 * D:]
        h = modulate(rmsnorm(x, norm_mix[l]), sh_m, sc_m)
        if l < N_A_LAYERS:
            mix = rglru_mixer(h, lru_w_in[l], lru_conv_w[l], lru_conv_b[l], lru_w_gates[l],
                              lru_b_gates[l], lru_lambda[l], lru_w_out[l])
        else:
            j = l - N_A_LAYERS
            mix = moba_mixer(h, attn_w_q[j], attn_w_o[j], kv[0], kv[1], kv[2], rel_bias)
        x = x + g_m[:, None, :] * mix
        h = modulate(rmsnorm(x, norm_ffn[l]), sh_f, sc_f)
        x = x + g_f[:, None, :] * swiglu(h, ffn_w_gate[l], ffn_w_up[l], ffn_w_down[l])
    return rmsnorm(x, final_norm)
```

```python
import contextlib
import numpy as np
import ml_dtypes
import concourse.bass as bass
import concourse.mybir as mybir
from concourse.bass_utils import run_bass_kernel_spmd

F32 = mybir.dt.float32
BF16 = mybir.dt.bfloat16
AF = mybir.ActivationFunctionType
ALU = mybir.AluOpType
AX = mybir.AxisListType

D = 1024
NCH = 8
TOK = 2048
NT = 4
TS = 512
LW = 1280
LCH = 10
DFF = 2816
FCH = 22
NBLK = 8
BLK = 256
EPS = 1e-6
NEG = -30000.0
SLOT = 4096
NSLOT = 6


class Eng:
    def __init__(self, nc, eng, sem):
        self.e = eng
        self.sem = sem
        self.n = 0
        self.seen = {}

    def wait_tok(self, sem, val):
        key = id(sem)
        if self.seen.get(key, (None, 0))[1] >= val:
            return
        self.seen[key] = (sem, val)
        self.e.wait_ge(sem, val)

    def wait_set(self, d):
        for sem, val in d.values():
            self.wait_tok(sem, val)

    def done(self, ins):
        self.n += 1
        ins.then_inc(self.sem, 1)
        return (self.sem, self.n)


class Buf:
    __slots__ = ("w", "r", "sem", "semv")

    def __init__(self, sem=None):
        self.w = {}
        self.r = {}
        self.sem = sem
        self.semv = 0


def _add(d, tok):
    sem, val = tok
    k = id(sem)
    if k not in d or d[k][1] < val:
        d[k] = (sem, val)


class KB:
    def __init__(self, nc, es):
        self.nc = nc
        self.es = es
        mk = lambda n: es.enter_context(nc.semaphore(n))
        self.pe = Eng(nc, nc.tensor, mk("s_pe"))
        self.act = Eng(nc, nc.scalar, mk("s_act"))
        self.dve = Eng(nc, nc.vector, mk("s_dve"))
        self.pool = Eng(nc, nc.gpsimd, mk("s_pool"))
        self.sp = Eng(nc, nc.sync, mk("s_sp"))
        self.nsem = 0

    def dsem(self):
        self.nsem += 1
        return self.es.enter_context(self.nc.semaphore("d%d" % self.nsem))

    def dbuf(self):
        return Buf(self.dsem())

    def _pre(self, eng, R, W):
        for b in R:
            eng.wait_set(b.w)
        for b in W:
            eng.wait_set(b.w)
            eng.wait_set(b.r)

    def _post(self, tok, R, W):
        for b in R:
            _add(b.r, tok)
        for b in W:
            b.r = {}
            _add(b.w, tok)

    def op(self, eng, fn, R=(), W=()):
        self._pre(eng, R, W)
        tok = eng.done(fn())
        self._post(tok, R, W)
        return tok

    def mm(self, mms, R=(), W=()):
        eng = self.pe
        self._pre(eng, R, W)
        n = len(mms)
        ins = None
        for i, (o, l, r) in enumerate(mms):
            ins = self.nc.tensor.matmul(o, l, r, start=(i == 0), stop=(i == n - 1))
        tok = eng.done(ins)
        self._post(tok, R, W)
        return tok

    def dma(self, q, out, in_, R=(), W=(), sb=None):
        if sb is None:
            sb = W[0] if (W and W[0].sem is not None) else R[0]
        self._pre(q, R, W)
        ins = q.e.dma_start(out=out, in_=in_)
        sb.semv += 16
        ins.then_inc(sb.sem, 16)
        tok = (sb.sem, sb.semv)
        self._post(tok, R, W)
        return tok

    def barrier(self, bufs_tokens=()):
        engs = [self.pe, self.act, self.dve, self.pool, self.sp]
        for e in engs:
            for o in engs:
                if o is not e and o.n > 0:
                    e.wait_tok(o.sem, o.n)
        for e in engs:
            for (sem, val) in bufs_tokens:
                e.wait_tok(sem, val)


class WStream:
    def __init__(self, kb, ring, nslot):
        self.kb = kb
        self.ring = ring
        self.nslot = nslot
        self.bufs = [kb.dbuf() for _ in range(nslot)]
        self.units = []
        self.issued = 0
        self.taken = 0

    def plan(self, units):
        self.units.extend(units)
        self._fill()

    def _view(self, s, a, b):
        return self.ring[:, s, 0:a * b].rearrange("p (a b) -> p a b", a=a)

    def _fill(self):
        while self.issued < len(self.units) and self.issued < self.taken + self.nslot:
            src, a, b = self.units[self.issued]
            s = self.issued % self.nslot
            self.kb.dma(self.kb.pool, self._view(s, a, b), src, W=[self.bufs[s]])
            self.issued += 1

    def get(self, k=0):
        i = self.taken + k
        assert i < self.issued, "weight stream underflow"
        src, a, b = self.units[i]
        s = i % self.nslot
        return self._view(s, a, b), self.bufs[s]

    def release(self):
        self.taken += 1
        self._fill()


def build_program(stop_after=99, debug=False, no_cc=False):
    nc = bass.Bass("TRN2", target_bir_lowering=False)

    def din(name, shape, dt=F32):
        return nc.dram_tensor(name, list(shape), dt, kind="ExternalInput").ap()

    x_own = din("x_own", [TOK, D])
    x_halo = din("x_halo", [128, NCH, 24])
    halo_flag = din("halo_flag", [128, 24])
    c_T = din("c_T", [128, NCH])
    mod_w = din("mod_w", [2, 12, 128, 4096])
    mod_b_T = din("mod_b_T", [128, 2, 48])
    norm_mix_T = din("norm_mix_T", [128, 2, NCH])
    norm_ffn_T = din("norm_ffn_T", [128, 2, NCH])
    w_in = din("w_in", [10, 128, 2048])
    conv_w_T = din("conv_w_T", [128, 4, LCH])
    conv_b_T = din("conv_b_T", [128, LCH])
    w_gates = din("w_gates", [5, 128, 1024])
    b_gates_T = din("b_gates_T", [128, 2, LCH])
    lambda_T = din("lambda_T", [128, LCH])
    w_out = din("w_out", [4, 128, 2560])
    kv_mod_w = din("kv_mod_w", [4, 128, 4096])
    kv_mod_b_T = din("kv_mod_b_T", [128, 16])
    kv_norm_T = din("kv_norm_T", [128, NCH])
    w_kv = din("w_kv", [2, 128, 4096])
    w_q = din("w_q", [2, 128, 4096])
    w_o = din("w_o", [2, 128, 4096])
    ffn_wg = din("ffn_wg", [2, 6, 128, 4096])
    ffn_wu = din("ffn_wu", [2, 6, 128, 4096])
    ffn_wd = din("ffn_wd", [2, 4, 2, 128, 2816])
    final_norm_T = din("final_norm_T", [128, NCH])
    bias31 = din("bias31", [128, 8])
    sbias = din("sbias", [8, 128, 5, 2, 256])
    vbias = din("vbias", [128, NBLK, 32])
    ownflag = din("ownflag", [128, NBLK, 32])
    rankflag = din("rankflag", [128, 4])
    ident_d = din("ident", [128, 128])
    esel_d = din("esel", [32, 32, 128])
    out_own = nc.dram_tensor("out_own", [TOK, D], F32, kind="ExternalOutput").ap()
    if debug:
        dbg_out = nc.dram_tensor("dbg", [128, NCH, TOK], F32, kind="ExternalOutput").ap()

    st_in = nc.dram_tensor("st_in", [128, 160], F32)
    st_out = nc.dram_tensor("st_out", [4 * 128, 160], F32)
    kvi = [[nc.dram_tensor("kvi%d_%d" % (t_, q), [256, 512], BF16) for q in range(4)] for t_ in range(NT)]
    kvo = [[nc.dram_tensor("kvo%d_%d" % (t_, q), [4 * 256, 512], BF16) for q in range(4)] for t_ in range(NT)]
    km_in = nc.dram_tensor("km_in", [512, 8], F32)
    km_out = nc.dram_tensor("km_out", [4 * 512, 8], F32)

    with contextlib.ExitStack() as es:
        kb = KB(nc, es)
        pe, act, dve, pool, sp = kb.pe, kb.act, kb.dve, kb.pool, kb.sp
        V_, A_, P_, G_ = nc.vector, nc.scalar, nc.tensor, nc.gpsimd
        cc_sems = [es.enter_context(nc.semaphore("cc%d" % i)) for i in range(1)]
        cc_kv = [[es.enter_context(nc.semaphore("cck%d_%d" % (t_, q))) for q in range(4)] for t_ in range(NT)]

        def sb(stack, name, shape, dt):
            return stack.enter_context(nc.sbuf_tensor(name, list(shape), dt))

        X = sb(es, "X", [128, NCH, TOK], F32)
        XB = [[Buf() for _ in range(NT)] for _ in range(NCH)]
        def mk_ws(stack, nslot, tag):
            ring_ = sb(stack, "ring" + tag, [128, nslot, SLOT], BF16)
            return WStream(kb, ring_, nslot)

        ident = sb(es, "identf", [128, 128], F32)
        identb = sb(es, "identb", [128, 128], BF16)
        onesb = sb(es, "onesb", [128, 128], BF16)
        cst = sb(es, "cst", [128, 8], F32)
        vec = sb(es, "vec", [128, 512], F32)
        vecB = kb.dbuf()
        cB = Buf()
        V_CS = 0
        V_MOD = [8, 56]
        V_KVM = 104
        V_NM = 120
        V_NF = 136
        V_KVN = 152
        V_FN = 160
        V_MODB = 168
        V_KVB = 264
        V_GS = 280
        V_CW = 320
        V_CB = 360
        V_BG = 370
        V_LAM = 390
        V_B31 = 400
        V_RF = 408
        V_HF = 412
        csb = sb(es, "csb", [128, NCH, 1], BF16)
        hsem_bufs = []

        psf = [es.enter_context(nc.psum_tensor("psf%d" % i, [128, 512], F32)) for i in range(7)]
        psb = es.enter_context(nc.psum_tensor("psb", [128, 1024], BF16))
        PB = [Buf() for _ in range(7)]
        PBb = Buf()
        rot = {"i": 0}

        def nbank(lst=(0, 1, 2, 3, 4, 5, 6)):
            i = lst[rot["i"] % len(lst)]
            rot["i"] += 1
            return psf[i], PB[i]

        s0 = contextlib.ExitStack()
        s0.__enter__()
        ws = mk_ws(s0, 4, "0")
        xst = sb(s0, "xst0", [128, 2, D], F32)
        xstB = [kb.dbuf(), kb.dbuf()]
        def ld(dst_cols, src, n):
            kb.dma(sp, vec[:, dst_cols:dst_cols + n], src, W=[vecB])

        ld(V_CS, c_T[:, :], 8)
        ld(V_NM, norm_mix_T.rearrange("p a b -> p (a b)"), 16)
        ld(V_NF, norm_ffn_T.rearrange("p a b -> p (a b)"), 16)
        ld(V_KVN, kv_norm_T[:, :], 8)
        ld(V_FN, final_norm_T[:, :], 8)
        ld(V_MODB, mod_b_T.rearrange("p a b -> p (a b)"), 96)
        ld(V_KVB, kv_mod_b_T[:, :], 16)
        ld(V_CW, conv_w_T.rearrange("p a b -> p (a b)"), 40)
        ld(V_CB, conv_b_T[:, :], 10)
        ld(V_BG, b_gates_T.rearrange("p a b -> p (a b)"), 20)
        ld(V_LAM, lambda_T[:, :], 10)
        ld(V_B31, bias31[:, :], 8)
        ld(V_RF, rankflag[:, :], 4)
        ld(V_HF, halo_flag[:, :], 24)
        identB = kb.dbuf()
        kb.dma(sp, ident[:], ident_d[:, :], W=[identB])
        kb.op(pool, lambda: G_.memset(cst[:, 0:1], EPS), W=[cB])
        kb.op(pool, lambda: G_.memset(cst[:, 1:2], 1.0), W=[cB])
        kb.op(pool, lambda: G_.memset(cst[:, 2:3], 0.0), W=[cB])
        kb.op(pool, lambda: G_.memset(onesb[:], 1.0 / D), W=[cB])
        kb.op(dve, lambda: V_.tensor_copy(out=identb[:], in_=ident[:]), R=[identB], W=[cB])
        kb.op(act, lambda: A_.activation(out=vec[:, V_CS:V_CS + 8], in_=vec[:, V_CS:V_CS + 8], func=AF.Silu), W=[vecB])
        kb.op(dve, lambda: V_.tensor_copy(out=csb[:, :, 0], in_=vec[:, V_CS:V_CS + 8]), R=[vecB], W=[cB])
        kb.op(act, lambda: A_.activation(out=vec[:, V_LAM:V_LAM + 10], in_=vec[:, V_LAM:V_LAM + 10], func=AF.Exp, scale=-1.0), W=[vecB])
        kb.op(act, lambda: A_.activation(out=vec[:, V_LAM:V_LAM + 10], in_=vec[:, V_LAM:V_LAM + 10], func=AF.Ln, bias=cst[:, 1:2]), R=[cB], W=[vecB])
        kb.op(dve, lambda: V_.tensor_scalar(out=vec[:, V_LAM:V_LAM + 10], in0=vec[:, V_LAM:V_LAM + 10], scalar1=-8.0, scalar2=None, op0=ALU.mult), W=[vecB])

        def mod_units(wap, ncols):
            return [(wap[u].rearrange("p (a b) -> p a b", a=8), 8, 512) for u in range(ncols // 512)]

        vecB2 = Buf()

        def mod_unit_work(u, dst_col, bias_col, late):
            wv, wB = ws.get()
            bank, bB = nbank()
            for m in range(4):
                kb.mm([(bank[:, m:m + 1], wv[:, kc, m * 128:(m + 1) * 128], csb[:, kc, :]) for kc in range(NCH)], R=[wB, cB], W=[bB])
            ws.release()
            j0 = u * 4
            kb.op(dve, lambda: V_.tensor_tensor(out=vec[:, dst_col + j0:dst_col + j0 + 4], in0=bank[:, 0:4],
                                                in1=vec[:, bias_col + j0:bias_col + j0 + 4], op=ALU.add),
                  R=[bB, vecB], W=[vecB2 if late else vecB])

        ws.plan([(mod_w[0][u].rearrange("p (a b) -> p a b", a=8), 8, 512) for u in range(4)])
        for u in range(4):
            mod_unit_work(u, V_MOD[0], V_MODB, False)
        deferred = [(mod_w[0][u], u, V_MOD[0], V_MODB) for u in range(4, 12)]
        deferred += [(mod_w[1][u], u, V_MOD[1], V_MODB + 48) for u in range(12)]
        deferred += [(kv_mod_w[u], u, V_KVM, V_KVB) for u in range(4)]

        def deferred_for(tt, hd):
            sidx = tt * 5 + hd
            lo = min(2 * sidx, 4 + sidx)
            hi = min(2 * (sidx + 1), 4 + sidx + 1)
            return deferred[lo:min(hi, len(deferred))]

        def mk_gs(dst, normcol, sccol, late):
            kb.op(dve, lambda: V_.scalar_tensor_tensor(out=vec[:, dst:dst + 8], in0=vec[:, sccol:sccol + 8], scalar=1.0,
                                                       in1=vec[:, normcol:normcol + 8], op0=ALU.add, op1=ALU.mult),
                  R=[vecB2] if late else [], W=[vecB])

        mk_gs(V_GS + 0, V_NM + 0, V_MOD[0] + 8, False)

        def col(c0, j):
            return vec[:, c0 + j:c0 + j + 1]

        for t16 in range(TOK // 128):
            s = t16 % 2
            kb.dma(sp, xst[:, s, :], x_own[t16 * 128:(t16 + 1) * 128, :], W=[xstB[s]])
            tt = t16 // 4
            for half in range(2):
                bank, bB = nbank()
                kb._pre(pe, [xstB[s], identB], [bB])
                ins = None
                for j in range(4):
                    c = half * 4 + j
                    ins = P_.transpose(bank[:, j * 128:(j + 1) * 128], xst[:, s, c * 128:(c + 1) * 128], ident[:])
                tok = pe.done(ins)
                kb._post(tok, [xstB[s], identB], [bB])
                dst = X[:, half * 4:half * 4 + 4, t16 * 128:(t16 + 1) * 128]
                srcv = bank[:].rearrange("p (a b) -> p a b", a=4)
                Wb = [XB[half * 4 + j][tt] for j in range(4)]
                if half == 0:
                    kb.op(act, lambda: A_.activation(out=dst, in_=srcv, func=AF.Copy), R=[bB], W=Wb)
                else:
                    kb.op(dve, lambda: V_.tensor_copy(out=dst, in_=srcv), R=[bB], W=Wb)

        kb.barrier([t_ for b_ in xstB for t_ in b_.w.values()])
        s0.__exit__(None, None, None)
        with contextlib.ExitStack() as sa:
            ws = mk_ws(sa, 4, "A")
            h = sb(sa, "h", [128, NCH, TS], BF16)
            hB = [Buf() for _ in range(NCH)]
            sq, sqB = h, hB
            rstd = sb(sa, "rstd", [128, TS], F32)
            rstdB = Buf()
            tmpn = sb(sa, "tmpn", [128, 2, TS], F32)
            tmpB = [Buf(), Buf()]
            zact = sb(sa, "zact", [128, FCH, TS], BF16)
            zB = [Buf() for _ in range(FCH)]
            hh = sb(sa, "hh", [128, NCH, 24], BF16)
            hhB = Buf()
            xh = sb(sa, "xh", [128, NCH, 24], F32)
            xhB = kb.dbuf()
            xbrh = sb(sa, "xbrh", [128, LCH, 24], F32)
            xbrhB = Buf()
            xbr = sb(sa, "xbr", [128, 2, 2, 259], F32)
            xbrB = [Buf(), Buf()]
            xc = sb(sa, "xc", [128, 2, TS], F32)
            xcB = [Buf(), Buf()]
            xcb = sb(sa, "xcb", [128, 2, TS], BF16)
            xcbB = [Buf(), Buf()]
            tmp5 = sb(sa, "tmp5", [128, 2, 5, TS], F32)
            tmp5B = [[Buf() for _ in range(5)] for _ in range(2)]
            stt_ = sb(sa, "stt", [128, 2, NBLK, LCH], F32)
            sttB = kb.dbuf()
            stall = sb(sa, "stall", [128, 4, 2, NBLK, LCH], F32)
            stallB = kb.dbuf()
            Hst = sb(sa, "Hst", [128, NBLK, LCH], F32)
            HstB = Buf()
            Hrun = sb(sa, "Hrun", [128, LCH], F32)
            HrunB = Buf()
            Aall = sb(sa, "Aall", [128, 4, NBLK, LCH], F32)
            AallB = Buf()
            kmean = sb(sa, "kmean", [128, 4, NBLK], F32)
            kmB = kb.dbuf()
            ksb = zact[:, 0:4, :]
            vsb = zact[:, 4:8, :]

            kb.op(pool, lambda: G_.memset(stt_[:], 0.0), W=[sttB])
            kb.op(pool, lambda: G_.memset(Hst[:], 0.0), W=[HstB])
            kb.op(pool, lambda: G_.memset(Hrun[:], 0.0), W=[HrunB])

            def norm_tile(xsrc, xbufs, n, gs_col, sh_col, dst, dstB, out_dt_is_bf=True):
                for c in range(NCH):
                    if c % 2 == 0:
                        kb.op(act, lambda: A_.activation(out=sq[:, c, 0:n], in_=xsrc(c), func=AF.Square), R=[xbufs[c]], W=[sqB[c]])
                    else:
                        kb.op(dve, lambda: V_.tensor_tensor(out=sq[:, c, 0:n], in0=xsrc(c), in1=xsrc(c), op=ALU.mult), R=[xbufs[c]], W=[sqB[c]])
                bank, bB = nbank()
                kb.mm([(bank[:, 0:n], onesb[:], sq[:, c, 0:n]) for c in range(NCH)], R=sqB + [cB], W=[bB])
                kb.op(act, lambda: A_.activation(out=rstd[:, 0:n], in_=bank[:, 0:n], func=AF.Sqrt, bias=cst[:, 0:1]), R=[bB, cB], W=[rstdB])
                kb.op(dve, lambda: V_.reciprocal(out=rstd[:, 0:n], in_=rstd[:, 0:n]), W=[rstdB])
                for c in range(NCH):
                    tb = c % 2
                    if sh_col is None:
                        kb.op(dve, lambda: V_.scalar_tensor_tensor(out=dst(c), in0=xsrc(c), scalar=col(gs_col, c), in1=rstd[:, 0:n],
                                                                   op0=ALU.mult, op1=ALU.mult), R=[xbufs[c], rstdB, vecB], W=[dstB[c]])
                    else:
                        kb.op(dve, lambda: V_.scalar_tensor_tensor(out=tmpn[:, tb, 0:n], in0=xsrc(c), scalar=col(gs_col, c), in1=rstd[:, 0:n],
                                                                   op0=ALU.mult, op1=ALU.mult), R=[xbufs[c], rstdB, vecB], W=[tmpB[tb]])
                        if c % 2 == 0:
                            kb.op(act, lambda: A_.activation(out=dst(c), in_=tmpn[:, tb, 0:n], func=AF.Identity, bias=col(sh_col, c)),
                                  R=[tmpB[tb], vecB], W=[dstB[c]])
                        else:
                            kb.op(act, lambda: A_.activation(out=dst(c), in_=tmpn[:, tb, 0:n], func=AF.Identity, bias=col(sh_col, c)),
                                  R=[tmpB[tb], vecB], W=[dstB[c]])

            def xtile(tt):
                return (lambda c: X[:, c, tt * TS:(tt + 1) * TS]), [XB[c][tt] for c in range(NCH)]

            kb.dma(sp, xh[:], x_halo[:, :, :], W=[xhB])
            norm_tile(lambda c: xh[:, c, :], [xhB] * NCH, 24, V_GS + 0, V_MOD[0] + 0, lambda c: hh[:, c, :], [hhB] * NCH)

            CW = lambda k, ch: col(V_CW + k * LCH, ch)

            zf = zact[:, 10:19, :].rearrange("p a b -> p (a b)").bitcast(F32)
            xbr_s = [xbr, zf[:, 0:1036].rearrange("p (c s n) -> p c s n", c=2, s=2)]
            xc_s = [xc, zf[:, 1036:2060].rearrange("p (c n) -> p c n", c=2)]
            xcb_s = [xcb, zact[:, 19:21, :]]
            xbrB_s = [xbrB, [Buf(), Buf()]]
            xcB_s = [xcB, [Buf(), Buf()]]
            xcbB_s = [xcbB, [Buf(), Buf()]]
            alias_bufs = xbrB_s[1] + xcB_s[1] + xcbB_s[1]

            def lru_tile(tt, full):
                xs_, xb_ = xtile(tt)
                norm_tile(xs_, xb_, TS, V_GS + 0, V_MOD[0] + 0, lambda c: h[:, c, :], hB)
                def g_units(hd):
                    u_ = [(w_gates[hd].rearrange("p (a b) -> p a b", a=2), 2, 512)]
                    if full:
                        u_.append((w_in[hd].rearrange("p (a b) -> p a b", a=8), 8, 256))
                    else:
                        for (wap_, u2_, dc_, bc_) in deferred_for(tt, hd):
                            u_.append((wap_.rearrange("p (a b) -> p a b", a=8), 8, 512))
                    return u_
                x_unit = lambda hd: (w_in[5 + hd].rearrange("p (a b) -> p a b", a=8), 8, 256)
                units = [x_unit(0)]
                for hd in range(5):
                    if hd + 1 < 5:
                        units.append(x_unit(hd + 1))
                    units += g_units(hd)
                if full:
                    for o2 in range(4):
                        units.append((w_out[o2].rearrange("p (a b) -> p a b", a=LCH), LCH, 256))
                ws.plan(units)
                for e_ in (pe, act, dve, pool):
                    for b_ in zB[10:21]:
                        e_.wait_set(b_.w)
                        e_.wait_set(b_.r)

                def X_part(hd):
                    s_ = hd % 2
                    xbr_, xc_, xcb_ = xbr_s[s_], xc_s[s_], xcb_s[s_]
                    xbrB_, xcB_, xcbB_ = xbrB_s[s_], xcB_s[s_], xcbB_s[s_]
                    wx, wxB = ws.get()
                    for cc in range(2):
                        ch = 2 * hd + cc
                        bank, bB = nbank()
                        kb.mm([(bank[:], wx[:, kc, cc * 128:(cc + 1) * 128], h[:, kc, :]) for kc in range(NCH)], R=hB + [wxB], W=[bB])
                        if tt == 0 and not full:
                            bank2, bB2 = nbank()
                            kb.mm([(bank2[:, 0:24], wx[:, kc, cc * 128:(cc + 1) * 128], hh[:, kc, :]) for kc in range(NCH)], R=[hhB, wxB], W=[bB2])
                            kb.op(dve, lambda: V_.tensor_tensor(out=xbrh[:, ch, :], in0=bank2[:, 0:24], in1=vec[:, V_HF:V_HF + 24], op=ALU.mult),
                                  R=[bB2, vecB], W=[xbrhB])
                        kb.op(act, lambda: A_.activation(out=xbr_[:, cc, :, 3:259], in_=bank[:].rearrange("p (s n) -> p s n", s=2), func=AF.Copy),
                              R=[bB], W=[xbrB_[cc]])
                        kb.op(pool, lambda: G_.tensor_copy(out=xbr_[:, cc, :, 0:3],
                                                           in_=xbrh[:, ch, 6 * tt:6 * tt + 6].rearrange("p (s n) -> p s n", s=2)),
                              R=[xbrhB], W=[xbrB_[cc]])
                        xcv = xc_[:, cc, :].rearrange("p (s n) -> p s n", s=2)
                        kb.op(act, lambda: A_.activation(out=xcv, in_=xbr_[:, cc, :, 3:259], func=AF.Identity, scale=CW(3, ch), bias=col(V_CB, ch)),
                              R=[xbrB_[cc], vecB], W=[xcB_[cc]])
                        for k in range(3):
                            kb.op(dve, lambda: V_.scalar_tensor_tensor(out=xcv, in0=xbr_[:, cc, :, k:k + 256], scalar=CW(k, ch), in1=xcv,
                                                                       op0=ALU.mult, op1=ALU.add), R=[xbrB_[cc], vecB], W=[xcB_[cc]])
                        kb.op(dve, lambda: V_.tensor_copy(out=xcb_[:, cc, :], in_=xc_[:, cc, :]), R=[xcB_[cc]], W=[xcbB_[cc]])
                    ws.release()

                def G_part(hd):
                    s_ = hd % 2
                    xc_, xcb_ = xc_s[s_], xcb_s[s_]
                    xcB_, xcbB_ = xcB_s[s_], xcbB_s[s_]
                    wg, wgB = ws.get()
                    gate_banks = []
                    for cc in range(2):
                        bank_r, bBr = nbank()
                        kb.mm([(bank_r[:], wg[:, kc, cc * 128:(cc + 1) * 128], xcb_[:, kc, :]) for kc in range(2)], R=xcbB_ + [wgB], W=[bBr])
                        bank_i, bBi = nbank()
                        kb.mm([(bank_i[:], wg[:, kc, 256 + cc * 128:256 + (cc + 1) * 128], xcb_[:, kc, :]) for kc in range(2)], R=xcbB_ + [wgB], W=[bBi])
                        gate_banks.append((bank_r, bBr, bank_i, bBi))
                    ws.release()
                    if full:
                        wy, wyB = ws.get()
                    T = lambda cc, q: tmp5[:, cc, q, :]
                    TB = lambda cc, q: tmp5B[cc][q]
                    for cc in range(2):
                        ch = 2 * hd + cc
                        bank_r, bBr, bank_i, bBi = gate_banks[cc]
                        if full:
                            kb.op(act, lambda: A_.activation(out=T(cc, 0), in_=bank_r[:], func=AF.Sigmoid, bias=col(V_BG, ch)), R=[bBr, vecB], W=[TB(cc, 0)])
                        else:
                            for seg in range(2):
                                blk = 2 * tt + seg
                                sl = slice(seg * 256, (seg + 1) * 256)
                                kb.op(act, lambda: A_.activation(out=T(cc, 0)[:, sl], in_=bank_r[:, sl], func=AF.Sigmoid, bias=col(V_BG, ch),
                                                                 accum_out=stt_[:, 0, blk, ch:ch + 1]), R=[bBr, vecB], W=[TB(cc, 0), sttB])
                        kb.op(act, lambda: A_.activation(out=T(cc, 1), in_=bank_i[:], func=AF.Sigmoid, bias=col(V_BG + LCH, ch)), R=[bBi, vecB], W=[TB(cc, 1)])
                    for cc in range(2):
                        ch = 2 * hd + cc
                        kb.op(act, lambda: A_.activation(out=T(cc, 2), in_=T(cc, 0), func=AF.Exp, scale=col(V_LAM, ch)), R=[TB(cc, 0), vecB], W=[TB(cc, 2)])
                        kb.op(dve, lambda: V_.tensor_tensor(out=T(cc, 0), in0=T(cc, 2), in1=T(cc, 2), op=ALU.mult), R=[TB(cc, 2)], W=[TB(cc, 0)])
                        kb.op(dve, lambda: V_.tensor_tensor(out=T(cc, 3), in0=T(cc, 1), in1=xc_[:, cc, :], op=ALU.mult), R=[TB(cc, 1), xcB_[cc]], W=[TB(cc, 3)])
                    for cc in range(2):
                        kb.op(act, lambda: A_.activation(out=T(cc, 0), in_=T(cc, 0), func=AF.Sqrt, scale=-1.0, bias=cst[:, 1:2]), R=[cB], W=[TB(cc, 0)])
                    for cc in range(2):
                        ch = 2 * hd + cc
                        kb.op(dve, lambda: V_.tensor_tensor(out=T(cc, 3), in0=T(cc, 3), in1=T(cc, 0), op=ALU.mult), R=[TB(cc, 0)], W=[TB(cc, 3)])
                        for seg in range(2):
                            blk = 2 * tt + seg
                            sl = slice(seg * 256, (seg + 1) * 256)
                            if full:
                                kb.op(dve, lambda: V_.tensor_tensor_scan(out=T(cc, 4)[:, sl], data0=T(cc, 2)[:, sl], data1=T(cc, 3)[:, sl],
                                                                         initial=Hst[:, blk, ch:ch + 1], op0=ALU.mult, op1=ALU.add),
                                      R=[TB(cc, 2), TB(cc, 3), HstB], W=[TB(cc, 4)])
                            else:
                                kb.op(dve, lambda: V_.tensor_tensor_scan(out=T(cc, 4)[:, sl], data0=T(cc, 2)[:, sl], data1=T(cc, 3)[:, sl],
                                                                         initial=0.0, op0=ALU.mult, op1=ALU.add),
                                      R=[TB(cc, 2), TB(cc, 3)], W=[TB(cc, 4)])
                                kb.op(pool, lambda: G_.tensor_copy(out=stt_[:, 1, blk, ch:ch + 1], in_=T(cc, 4)[:, seg * 256 + 255:seg * 256 + 256]),
                                      R=[TB(cc, 4)], W=[sttB])
                    if not full:
                        for (wap_, u_, dc_, bc_) in deferred_for(tt, hd):
                            mod_unit_work(u_, dc_, bc_, True)
                    if full and hd == 1 and cc_state['pending'] is not None:
                        issue_kv_cc(cc_state['pending'])
                        cc_state['pending'] = None
                    if full:
                        for cc in range(2):
                            ch = 2 * hd + cc
                            banky, bBy = nbank()
                            kb.mm([(banky[:], wy[:, kc, cc * 128:(cc + 1) * 128], h[:, kc, :]) for kc in range(NCH)], R=hB + [wyB], W=[bBy])
                            kb.op(act, lambda: A_.activation(out=T(cc, 1), in_=banky[:], func=AF.Gelu_apprx_tanh), R=[bBy], W=[TB(cc, 1)])
                            kb.op(dve, lambda: V_.tensor_tensor(out=zact[:, ch, :], in0=T(cc, 4), in1=T(cc, 1), op=ALU.mult), R=[TB(cc, 4), TB(cc, 1)], W=[zB[ch]])
                        ws.release()

                X_part(0)
                for hd in range(5):
                    if hd + 1 < 5:
                        X_part(hd + 1)
                    G_part(hd)
                for k_ in range(10, 21):
                    for b_ in alias_bufs:
                        for t_ in list(b_.w.values()) + list(b_.r.values()):
                            _add(zB[k_].r, t_)
                if full:
                    for o2 in range(4):
                        wo, woB = ws.get()
                        for oo in range(2):
                            o = o2 * 2 + oo
                            bank, bB = nbank()
                            kb.mm([(bank[:], wo[:, kc, oo * 128:(oo + 1) * 128], zact[:, kc, :]) for kc in range(LCH)], R=zB[0:LCH] + [woB], W=[bB])
                            xs = X[:, o, tt * TS:(tt + 1) * TS]
                            kb.op(dve, lambda: V_.scalar_tensor_tensor(out=xs, in0=bank[:], scalar=col(V_MOD[0] + 16, o), in1=xs, op0=ALU.mult, op1=ALU.add),
                                  R=[bB, vecB], W=[XB[o][tt]])
                        ws.release()

            for tt in range(NT if stop_after >= 1 else 0):
                lru_tile(tt, False)

            if stop_after >= 1:
                mk_gs(V_GS + 8, V_NF + 0, V_MOD[0] + 32, True)
                mk_gs(V_GS + 16, V_NM + 8, V_MOD[1] + 8, True)
                mk_gs(V_GS + 24, V_NF + 8, V_MOD[1] + 32, True)
                mk_gs(V_GS + 32, V_KVN, V_KVM + 8, True)
            stinB = kb.dbuf()
            kb.dma(pool, st_in.ap(), stt_[:].rearrange("p a b c -> p (a b c)"), R=[sttB], W=[stinB])
            kb._pre(pool, [stinB], [])
            if not no_cc or no_cc == 2:
                pool.e.collective_compute("AllGather", ALU.bypass, replica_groups=[[0, 1, 2, 3], [4, 5, 6, 7]],
                                          ins=[st_in.ap().opt()], outs=[st_out.ap().opt()]).then_inc(cc_sems[0])
                pool.e.wait_ge(cc_sems[0], 1)
            kb.dma(pool, stall[:].rearrange("p r a b c -> p r (a b c)"), st_out.ap().rearrange("(r p) f -> p r f", p=128), W=[stallB])
            for r in range(4):
                for i in range(NBLK):
                    kb.op(dve, lambda: V_.tensor_tensor(out=Aall[:, r, i, :], in0=stall[:, r, 0, i, :], in1=vec[:, V_LAM:V_LAM + LCH], op=ALU.mult),
                          R=[stallB, vecB], W=[AallB])
            kb.op(act, lambda: A_.activation(out=Aall[:], in_=Aall[:], func=AF.Exp), W=[AallB])
            for i in range(NBLK):
                for r in range(4):
                    kb.op(dve, lambda: V_.scalar_tensor_tensor(out=Hst[:, i, :], in0=Hrun[:], scalar=col(V_RF, r), in1=Hst[:, i, :],
                                                               op0=ALU.mult, op1=ALU.add), R=[HrunB, vecB], W=[HstB])
                    kb.op(dve, lambda: V_.tensor_tensor(out=Hrun[:], in0=Hrun[:], in1=Aall[:, r, i, :], op=ALU.mult), R=[AallB], W=[HrunB])
                    kb.op(dve, lambda: V_.tensor_tensor(out=Hrun[:], in0=Hrun[:], in1=stall[:, r, 1, i, :], op=ALU.add), R=[stallB], W=[HrunB])

            def ffn_tile(l, tt, h_, hB_, sq_, zact_, zB_, normf):
                xs_, xb_ = xtile(tt)
                normf(xs_, xb_, TS, V_GS + l * 16 + 8, V_MOD[l] + 24, lambda c: h_[:, c, :], hB_)
                units = []
                groups = [(g * 512, 512) for g in range(5)] + [(2560, 256)]
                for (c0, w) in groups:
                    units.append((ffn_wg[l, c0 // 512][:, 0:8 * w].rearrange("p (a b) -> p a b", a=8), 8, w))
                    units.append((ffn_wu[l, c0 // 512][:, 0:8 * w].rearrange("p (a b) -> p a b", a=8), 8, w))
                for og in range(4):
                    units.append((ffn_wd[l, og, 0].rearrange("p (a b) -> p a b", a=11), 11, 256))
                    units.append((ffn_wd[l, og, 1].rearrange("p (a b) -> p a b", a=11), 11, 256))
                ws.plan(units)
                for (c0, w) in groups:
                    wgt, wgtB = ws.get(0)
                    wup, wupB = ws.get(1)
                    for m in range(w // 128):
                        fc = c0 // 128 + m
                        bg, bgB = nbank()
                        kb.mm([(bg[:], wgt[:, kc, m * 128:(m + 1) * 128], h_[:, kc, :]) for kc in range(NCH)], R=hB_ + [wgtB], W=[bgB])
                        bu, buB = nbank()
                        kb.mm([(bu[:], wup[:, kc, m * 128:(m + 1) * 128], h_[:, kc, :]) for kc in range(NCH)], R=hB_ + [wupB], W=[buB])
                        tb = fc % 2
                        kb.op(act, lambda: A_.activation(out=tmpn[:, tb, :], in_=bg[:], func=AF.Silu), R=[bgB], W=[tmpB[tb]])
                        kb.op(dve, lambda: V_.tensor_tensor(out=zact_[:, fc, :], in0=tmpn[:, tb, :], in1=bu[:], op=ALU.mult), R=[tmpB[tb], buB], W=[zB_[fc]])
                    ws.release()
                    ws.release()
                for og in range(4):
                    wd, wdB = ws.get(0)
                    wd2, wd2B = ws.get(1)
                    for oo in range(2):
                        o = og * 2 + oo
                        bank, bB = nbank()
                        kb.mm([(bank[:], (wd if kc < 11 else wd2)[:, kc % 11, oo * 128:(oo + 1) * 128], zact_[:, kc, :]) for kc in range(FCH)],
                              R=zB_ + [wdB, wd2B], W=[bB])
                        xs = X[:, o, tt * TS:(tt + 1) * TS]
                        kb.op(dve, lambda: V_.scalar_tensor_tensor(out=xs, in0=bank[:], scalar=col(V_MOD[l] + 40, o), in1=xs, op0=ALU.mult, op1=ALU.add),
                              R=[bB, vecB], W=[XB[o][tt]])
                    ws.release()
                    ws.release()

            kvinB = [kb.dbuf() for _ in range(NT)]
            kvinVB = [kb.dbuf() for _ in range(NT)]
            cc_state = {'pending': None}

            def issue_kv_cc(t_):
                kb._pre(pool, [kvinB[t_], kvinVB[t_]], [])
                if not no_cc:
                    for q in range(4):
                        pool.e.collective_compute('AllGather', ALU.bypass, replica_groups=[[0, 1, 2, 3], [4, 5, 6, 7]],
                                                  ins=[kvi[t_][q].ap().opt()], outs=[kvo[t_][q].ap().opt()]).then_inc(cc_kv[t_][q])

            kminB = kb.dbuf()

            def kv_tile(tt):
                xs_, xb_ = xtile(tt)
                norm_tile(xs_, xb_, TS, V_GS + 32, V_KVM + 0, lambda c: h[:, c, :], hB)
                ws.plan([(w_kv[g].rearrange("p (a b) -> p a b", a=8), 8, 512) for g in range(2)])
                wk, wkB = ws.get()
                for hk in range(4):
                    bank, bB = nbank()
                    kb.mm([(bank[:], wk[:, kc, hk * 128:(hk + 1) * 128], h[:, kc, :]) for kc in range(NCH)], R=hB + [wkB], W=[bB])
                    kb.op(act, lambda: A_.activation(out=ksb[:, hk, :], in_=bank[:], func=AF.Copy), R=[bB], W=[zB[hk]])
                ws.release()
                for pr in range(2):
                    _add(kvinB[tt].w, kb.dma(sp, kvi[tt][pr].ap().rearrange("(h d) t -> d h t", d=128), ksb[:, 2 * pr:2 * pr + 2, :],
                                             R=[zB[2 * pr], zB[2 * pr + 1]], sb=kvinB[tt]))
                wv, wvB = ws.get()
                for sub in range(4):
                    bank, bB = nbank()
                    kb.mm([(bank[:], h[:, kc, sub * 128:(sub + 1) * 128], wv[:, kc, :]) for kc in range(NCH)], R=hB + [wvB], W=[bB])
                    if sub % 2 == 0:
                        kb.op(act, lambda: A_.activation(out=vsb[:, sub, :], in_=bank[:], func=AF.Copy), R=[bB], W=[zB[4 + sub]])
                    else:
                        kb.op(dve, lambda: V_.tensor_copy(out=vsb[:, sub, :], in_=bank[:]), R=[bB], W=[zB[4 + sub]])
                ws.release()
                for sub in range(4):
                    n_ = tt * 4 + sub
                    for pr in range(2):
                        dstv = kvi[tt][2 + pr].ap()[:, sub * 128:(sub + 1) * 128].rearrange("(h p) d -> p h d", p=128)
                        _add(kvinVB[tt].w, kb.dma(sp, dstv, vsb[:, sub, pr * 256:(pr + 1) * 256].rearrange("p (h d) -> p h d", d=128), R=[zB[4 + sub]], sb=kvinVB[tt]))

            for tt in range(NT if stop_after >= 2 else 0):
                lru_tile(tt, True)
                ffn_tile(0, tt, h, hB, sq, zact, zB, norm_tile)
                kv_tile(tt)
                cc_state['pending'] = tt
            if cc_state['pending'] is not None:
                issue_kv_cc(cc_state['pending'])
                cc_state['pending'] = None

            if debug and stop_after <= 2:
                dbB = kb.dbuf()
                tk = kb.dma(sp, dbg_out[:, :, :], X[:], R=[b for row in XB for b in row], W=[dbB])
                sp.wait_tok(*tk)

            kb.barrier([t_ for b_ in kvinB + kvinVB for t_ in b_.w.values()])
        kb.barrier()
        if stop_after >= 3:
            with contextlib.ExitStack() as sbk:
                QT = sb(sbk, "QT", [128, 8, TOK], BF16)
                QTB = [[Buf() for _ in range(NT)] for _ in range(8)]
                with contextlib.ExitStack() as s1:
                    ws = mk_ws(s1, 4, "Q")
                    h = sb(s1, "h1", [128, NCH, TS], BF16)
                    hB = [Buf() for _ in range(NCH)]
                    sq, sqB = h, hB
                    rstd = sb(s1, "rstd1", [128, TS], F32)
                    rstdB = Buf()
                    tmpn = sb(s1, "tmpn1", [128, 2, TS], F32)
                    tmpB = [Buf(), Buf()]

                    def norm_tile1(xsrc, xbufs, n, gs_col, sh_col, dst, dstB):
                        for c in range(NCH):
                            if c % 2 == 0:
                                kb.op(act, lambda: A_.activation(out=sq[:, c, 0:n], in_=xsrc(c), func=AF.Square), R=[xbufs[c]], W=[sqB[c]])
                            else:
                                kb.op(dve, lambda: V_.tensor_tensor(out=sq[:, c, 0:n], in0=xsrc(c), in1=xsrc(c), op=ALU.mult), R=[xbufs[c]], W=[sqB[c]])
                        bank, bB = nbank()
                        kb.mm([(bank[:, 0:n], onesb[:], sq[:, c, 0:n]) for c in range(NCH)], R=sqB + [cB], W=[bB])
                        kb.op(act, lambda: A_.activation(out=rstd[:, 0:n], in_=bank[:, 0:n], func=AF.Sqrt, bias=cst[:, 0:1]), R=[bB, cB], W=[rstdB])
                        kb.op(dve, lambda: V_.reciprocal(out=rstd[:, 0:n], in_=rstd[:, 0:n]), W=[rstdB])
                        for c in range(NCH):
                            tb = c % 2
                            kb.op(dve, lambda: V_.scalar_tensor_tensor(out=tmpn[:, tb, 0:n], in0=xsrc(c), scalar=col(gs_col, c), in1=rstd[:, 0:n],
                                                                       op0=ALU.mult, op1=ALU.mult), R=[xbufs[c], rstdB, vecB], W=[tmpB[tb]])
                            if c % 2 == 0:
                                kb.op(act, lambda: A_.activation(out=dst(c), in_=tmpn[:, tb, 0:n], func=AF.Identity, bias=col(sh_col, c)),
                                      R=[tmpB[tb], vecB], W=[dstB[c]])
                            else:
                                kb.op(act, lambda: A_.activation(out=dst(c), in_=tmpn[:, tb, 0:n], func=AF.Identity, bias=col(sh_col, c)),
                                      R=[tmpB[tb], vecB], W=[dstB[c]])

                    qscale = 128.0 ** -0.5
                    for tt in range(NT):
                        xs_ = lambda c, tt=tt: X[:, c, tt * TS:(tt + 1) * TS]
                        norm_tile1(xs_, [XB[c][tt] for c in range(NCH)], TS, V_GS + 16, V_MOD[1] + 0, lambda c: h[:, c, :], hB)
                        ws.plan([(w_q[g].rearrange("p (a b) -> p a b", a=8), 8, 512) for g in range(2)])
                        for g in range(2):
                            wq_, wqB = ws.get()
                            for m in range(4):
                                hq = g * 4 + m
                                bank, bB = nbank()
                                kb.mm([(bank[:], wq_[:, kc, m * 128:(m + 1) * 128], h[:, kc, :]) for kc in range(NCH)], R=hB + [wqB], W=[bB])
                                if m % 2 == 0:
                                    kb.op(act, lambda: A_.activation(out=QT[:, hq, tt * TS:(tt + 1) * TS], in_=bank[:], func=AF.Copy, scale=qscale),
                                          R=[bB], W=[QTB[hq][tt]])
                                else:
                                    kb.op(dve, lambda: V_.tensor_scalar(out=QT[:, hq, tt * TS:(tt + 1) * TS], in0=bank[:], scalar1=qscale, scalar2=None, op0=ALU.mult),
                                          R=[bB], W=[QTB[hq][tt]])
                            ws.release()
                    kb.barrier()
                kb.barrier()
                with contextlib.ExitStack() as s2:
                    KT = sb(s2, "KT", [128, 4, TOK], BF16)
                    KTB = [kb.dbuf() for _ in range(NT)]
                    VV = sb(s2, "VV", [128, 64, 130], BF16)
                    VVB = [[kb.dbuf() for _ in range(NT)] for _ in range(4)]
                    Otok = sb(s2, "Otok", [128, 16, 128], BF16)
                    OtokB = Buf()
                    PT = sb(s2, "PT", [128, 4, 2, 256], BF16)
                    PTB = [Buf() for _ in range(4)]
                    selall = sb(s2, "selall", [128, 16, 32], F32)
                    Oacc = sb(s2, "Oacc", [128, 2, 2, 130], F32)
                    OaccB = [Buf(), Buf()]
                    sbs = sb(s2, "sbs", [128, 5, 2, 256], BF16)
                    sbsB = kb.dbuf()
                    kmf = sb(s2, "kmf", [128, 4, 32], F32)
                    kmfB = kb.dbuf()
                    kmT = sb(s2, "kmT", [128, 4, 32], BF16)
                    kmTB = Buf()
                    vbs = sb(s2, "vbs", [128, NBLK, 32], F32)
                    ofs = sb(s2, "ofs", [128, NBLK, 32], F32)
                    vbB = kb.dbuf()
                    ofB = kb.dbuf()
                    gm = sb(s2, "gm", [128, 2, 32], F32); gmB = [Buf(), Buf()]
                    top8 = sb(s2, "top8", [128, 2, 8], F32); t8B = [Buf(), Buf()]
                    selB = [Buf() for _ in range(NBLK)]
                    rec = sb(s2, "rec", [128, 2], F32); recB = Buf()

                    kb.dma(sp, vbs[:], vbias[:, :, :], W=[vbB])
                    kb.dma(sp, ofs[:], ownflag[:, :, :], W=[ofB])
                    kb.op(pool, lambda: G_.memset(VV[:, :, 128:130], 1.0), W=[b_ for row_ in VVB for b_ in row_])

                    for hq in range(8):
                        kvh = hq // 2
                        if hq % 2 == 0:
                            for t_ in range(NT):
                                if not no_cc:
                                    sp.wait_tok(cc_kv[t_][kvh // 2], 1)
                                kb.dma(sp, KT[:, :, t_ * TS:(t_ + 1) * TS],
                                       kvo[t_][kvh // 2].ap().rearrange("(r hh d) t -> hh d r t", r=4, hh=2)[kvh % 2], W=[KTB[t_]])
                            kb.op(dve, lambda: V_.tensor_reduce(out=kmf[:, 0, 0:32], in_=KT[:].rearrange("p r (i n) -> p (r i) n", n=BLK),
                                                                axis=AX.X, op=ALU.add), R=KTB, W=[kmfB])
                            kb.op(dve, lambda: V_.tensor_scalar(out=kmT[:, kvh, :], in0=kmf[:, 0, 0:32], scalar1=1.0 / BLK, scalar2=None, op0=ALU.mult),
                                  R=[kmfB], W=[kmTB])
                            for t_ in range(NT):
                                if not no_cc:
                                    sp.wait_tok(cc_kv[t_][2 + kvh // 2], 1)
                                for r_ in range(4):
                                    r0_ = r_ * 256 + (kvh % 2) * 128
                                    kb.dma(sp, VV[:, r_ * 16 + t_ * 4:r_ * 16 + t_ * 4 + 4, 0:128],
                                           kvo[t_][2 + kvh // 2].ap()[r0_:r0_ + 128, :].rearrange("p (n d) -> p n d", d=128), W=[VVB[r_][t_]])
                        kb.dma(pool, sbs[:], sbias[hq], W=[sbsB])
                        def gate_qblock(i):
                            for qt in (2 * i, 2 * i + 1):
                                tt = qt // 4
                                g_ = qt % 2
                                bank, bB = nbank((0, 1, 2))
                                kb.mm([(bank[:, 0:32], QT[:, hq, qt * 128:(qt + 1) * 128], kmT[:, kvh, :])], R=[QTB[hq][tt], kmTB], W=[bB])
                                kb.op(dve, lambda: V_.tensor_tensor(out=gm[:, g_, :], in0=bank[:, 0:32], in1=vbs[:, i, :], op=ALU.add), R=[bB, vbB], W=[gmB[g_]])
                                kb.op(dve, lambda: V_.max(out=top8[:, g_, :], in_=gm[:, g_, :]), R=[gmB[g_]], W=[t8B[g_]])
                                kb.op(dve, lambda: V_.tensor_scalar(out=top8[:, g_, 2:3], in0=top8[:, g_, 2:3], scalar1=-1e29, scalar2=None, op0=ALU.max), W=[t8B[g_]])
                                kb.op(dve, lambda: V_.tensor_scalar(out=gm[:, g_, :], in0=gm[:, g_, :], scalar1=top8[:, g_, 2:3], scalar2=None, op0=ALU.is_ge),
                                      R=[t8B[g_]], W=[gmB[g_]])
                                kb.op(pool, lambda: G_.tensor_tensor(out=selall[:, qt, :], in0=gm[:, g_, :], in1=ofs[:, i, :], op=ALU.add),
                                      R=[gmB[g_], ofB], W=[selB[i]])

                        gate_qblock(0)
                        items = []
                        for i in range(NBLK):
                            for ip in range(i + 1):
                                for r in range(4):
                                    if ip == i:
                                        slot = r
                                    elif ip == i - 1 and r == 3:
                                        slot = 4
                                    else:
                                        slot = None
                                    items.append((i, r, ip, slot, ip == 0 and r == 0, ip == i and r == 3))
                        st_ = {}

                        def emit_s(n):
                            i, r, ip, slot, first, last = items[n]
                            tt = i // 2
                            if first and i + 1 < NBLK:
                                gate_qblock(i + 1)
                            sbank, sbB_ = nbank((0, 1, 2, 3))
                            sv = sbank[:].rearrange("p (u n) -> p u n", u=2)
                            Rl = [KTB[ip // 2], QTB[hq][tt]] + ([sbsB, cB] if slot is not None else [])
                            kb._pre(pe, Rl, [sbB_])
                            ins = None
                            for u in range(2):
                                ins = P_.matmul(sv[:, u, :], KT[:, r, ip * 256 + u * 128:ip * 256 + (u + 1) * 128], QT[:, hq, i * 256:(i + 1) * 256],
                                                start=True, stop=(slot is None))
                                if slot is not None:
                                    ins = P_.matmul(sv[:, u, :], identb[:], sbs[:, slot, u, :], start=False, stop=True)
                            tok = pe.done(ins)
                            kb._post(tok, Rl, [sbB_])
                            p = n % 4
                            if slot is None:
                                kb.op(act, lambda: A_.activation(out=PT[:, p, :, :], in_=sv, func=AF.Exp, bias=col(V_B31, hq)), R=[sbB_, vecB], W=[PTB[p]])
                            else:
                                kb.op(act, lambda: A_.activation(out=PT[:, p, :, :], in_=sv, func=AF.Exp), R=[sbB_], W=[PTB[p]])

                        def emit_pv(n):
                            i, r, ip, slot, first, last = items[n]
                            j = r * 8 + ip
                            p = n % 4
                            par = i % 2
                            pvb, pvB = nbank((4, 5, 6))
                            kb._pre(pe, [PTB[p], VVB[r][ip // 2]], [pvB])
                            ins = None
                            for v in range(2):
                                for u in range(2):
                                    ins = P_.matmul(pvb[:, v * 256:v * 256 + 129], PT[:, p, u, v * 128:(v + 1) * 128], VV[:, r * 16 + ip * 2 + u, 0:129],
                                                    start=(u == 0), stop=(u == 1))
                            tok = pe.done(ins)
                            kb._post(tok, [PTB[p], VVB[r][ip // 2]], [pvB])
                            if first:
                                kb.op(pool, lambda: G_.memset(Oacc[:, par, :, :], 0.0), W=[OaccB[par]])
                            for v in range(2):
                                kb.op(dve, lambda: V_.scalar_tensor_tensor(out=Oacc[:, par, v, 0:129], in0=pvb[:, v * 256:v * 256 + 129],
                                                                           scalar=selall[:, 2 * i + v, j:j + 1], in1=Oacc[:, par, v, 0:129],
                                                                           op0=ALU.mult, op1=ALU.add), R=[pvB, selB[i]], W=[OaccB[par]])
                            if last:
                                for v in range(2):
                                    kb.op(dve, lambda: V_.reciprocal(out=rec[:, v:v + 1], in_=Oacc[:, par, v, 128:129]), R=[OaccB[par]], W=[recB])
                                    kb.op(dve, lambda: V_.tensor_scalar(out=Otok[:, 2 * i + v, :], in0=Oacc[:, par, v, 0:128], scalar1=rec[:, v:v + 1],
                                                                        scalar2=None, op0=ALU.mult), R=[OaccB[par], recB], W=[OtokB])

                        for n in range(len(items) + 1):
                            if n < len(items):
                                emit_s(n)
                            if n >= 3:
                                emit_pv(n - 3)
                        emit_pv(len(items) - 2)
                        emit_pv(len(items) - 1)
                        for g4 in range(4):
                            kb._pre(pe, [OtokB, cB], [PBb])
                            ins = None
                            for j4 in range(4):
                                qt = g4 * 4 + j4
                                ins = P_.transpose(psb[:, j4 * 128:(j4 + 1) * 128], Otok[:, qt, :], identb[:])
                            tok = pe.done(ins)
                            kb._post(tok, [OtokB, cB], [PBb])
                            kb.op(act, lambda: A_.activation(out=QT[:, hq, g4 * 512:(g4 + 1) * 512], in_=psb[:, 0:512], func=AF.Copy), R=[PBb], W=[QTB[hq][g4]])
                    kb.barrier()
                kb.barrier()
                s3 = contextlib.ExitStack()
                s3.__enter__()
                ws = mk_ws(s3, 4, "O")
                for tt in range(NT):
                    ws.plan([(w_o[g].rearrange("p (a b) -> p a b", a=8), 8, 512) for g in range(2)])
                    for g in range(2):
                        wo_, woB = ws.get()
                        for m in range(4):
                            o = g * 4 + m
                            bank, bB = nbank()
                            kb.mm([(bank[:], wo_[:, kc, m * 128:(m + 1) * 128], QT[:, kc, tt * TS:(tt + 1) * TS]) for kc in range(8)],
                                  R=[QTB[kc][tt] for kc in range(8)] + [woB], W=[bB])
                            xs = X[:, o, tt * TS:(tt + 1) * TS]
                            kb.op(dve, lambda: V_.scalar_tensor_tensor(out=xs, in0=bank[:], scalar=col(V_MOD[1] + 16, o), in1=xs, op0=ALU.mult, op1=ALU.add),
                                  R=[bB, vecB], W=[XB[o][tt]])
                        ws.release()
                kb.barrier()
                s3.__exit__(None, None, None)
            kb.barrier()
            with contextlib.ExitStack() as sc:
                ws = mk_ws(sc, 5, "C")
                xst = sb(sc, "xst2", [128, 2, D], F32)
                xstB = [kb.dbuf(), kb.dbuf()]
                h = sb(sc, "h2", [128, NCH, TS], BF16)
                hB = [Buf() for _ in range(NCH)]
                sq, sqB = h, hB
                rstd = sb(sc, "rstd2", [128, TS], F32)
                rstdB = Buf()
                tmpn = sb(sc, "tmpn2", [128, 2, TS], F32)
                tmpB = [Buf(), Buf()]
                zact = sb(sc, "zact2", [128, FCH, TS], BF16)
                zB = [Buf() for _ in range(FCH)]
                yn = sb(sc, "yn", [128, NCH, TS], F32)
                ynB = [Buf() for _ in range(NCH)]

                def norm_tile2(xsrc, xbufs, n, gs_col, sh_col, dst, dstB):
                    for c in range(NCH):
                        if c % 2 == 0:
                            kb.op(act, lambda: A_.activation(out=sq[:, c, 0:n], in_=xsrc(c), func=AF.Square), R=[xbufs[c]], W=[sqB[c]])
                        else:
                            kb.op(dve, lambda: V_.tensor_tensor(out=sq[:, c, 0:n], in0=xsrc(c), in1=xsrc(c), op=ALU.mult), R=[xbufs[c]], W=[sqB[c]])
                    bank, bB = nbank()
                    kb.mm([(bank[:, 0:n], onesb[:], sq[:, c, 0:n]) for c in range(NCH)], R=sqB + [cB], W=[bB])
                    kb.op(act, lambda: A_.activation(out=rstd[:, 0:n], in_=bank[:, 0:n], func=AF.Sqrt, bias=cst[:, 0:1]), R=[bB, cB], W=[rstdB])
                    kb.op(dve, lambda: V_.reciprocal(out=rstd[:, 0:n], in_=rstd[:, 0:n]), W=[rstdB])
                    for c in range(NCH):
                        tb = c % 2
                        if sh_col is None:
                            kb.op(dve, lambda: V_.scalar_tensor_tensor(out=dst(c), in0=xsrc(c), scalar=col(gs_col, c), in1=rstd[:, 0:n],
                                                                       op0=ALU.mult, op1=ALU.mult), R=[xbufs[c], rstdB, vecB], W=[dstB[c]])
                            continue
                        kb.op(dve, lambda: V_.scalar_tensor_tensor(out=tmpn[:, tb, 0:n], in0=xsrc(c), scalar=col(gs_col, c), in1=rstd[:, 0:n],
                                                                   op0=ALU.mult, op1=ALU.mult), R=[xbufs[c], rstdB, vecB], W=[tmpB[tb]])
                        if c % 2 == 0:
                            kb.op(act, lambda: A_.activation(out=dst(c), in_=tmpn[:, tb, 0:n], func=AF.Identity, bias=col(sh_col, c)),
                                  R=[tmpB[tb], vecB], W=[dstB[c]])
                        else:
                            kb.op(act, lambda: A_.activation(out=dst(c), in_=tmpn[:, tb, 0:n], func=AF.Identity, bias=col(sh_col, c)),
                                  R=[tmpB[tb], vecB], W=[dstB[c]])

                out_toks = []
                oi = 0
                for tt in range(NT):
                    ffn_tile_c = None
                    xs_ = lambda c, tt=tt: X[:, c, tt * TS:(tt + 1) * TS]
                    xb_ = [XB[c][tt] for c in range(NCH)]
                    l = 1
                    norm_tile2(xs_, xb_, TS, V_GS + l * 16 + 8, V_MOD[l] + 24, lambda c: h[:, c, :], hB)
                    units = []
                    groups = [(g * 512, 512) for g in range(5)] + [(2560, 256)]
                    for (c0, w) in groups:
                        units.append((ffn_wg[l, c0 // 512][:, 0:8 * w].rearrange("p (a b) -> p a b", a=8), 8, w))
                        units.append((ffn_wu[l, c0 // 512][:, 0:8 * w].rearrange("p (a b) -> p a b", a=8), 8, w))
                    for og in range(4):
                        units.append((ffn_wd[l, og, 0].rearrange("p (a b) -> p a b", a=11), 11, 256))
                        units.append((ffn_wd[l, og, 1].rearrange("p (a b) -> p a b", a=11), 11, 256))
                    ws.plan(units)
                    for (c0, w) in groups:
                        wgt, wgtB = ws.get(0)
                        wup, wupB = ws.get(1)
                        for m in range(w // 128):
                            fc = c0 // 128 + m
                            bg, bgB = nbank()
                            kb.mm([(bg[:], wgt[:, kc, m * 128:(m + 1) * 128], h[:, kc, :]) for kc in range(NCH)], R=hB + [wgtB], W=[bgB])
                            bu, buB = nbank()
                            kb.mm([(bu[:], wup[:, kc, m * 128:(m + 1) * 128], h[:, kc, :]) for kc in range(NCH)], R=hB + [wupB], W=[buB])
                            tb = fc % 2
                            kb.op(act, lambda: A_.activation(out=tmpn[:, tb, :], in_=bg[:], func=AF.Silu), R=[bgB], W=[tmpB[tb]])
                            kb.op(dve, lambda: V_.tensor_tensor(out=zact[:, fc, :], in0=tmpn[:, tb, :], in1=bu[:], op=ALU.mult), R=[tmpB[tb], buB], W=[zB[fc]])
                        ws.release()
                        ws.release()
                    for og in range(4):
                        wd, wdB = ws.get(0)
                        wd2, wd2B = ws.get(1)
                        for oo in range(2):
                            o = og * 2 + oo
                            bank, bB = nbank()
                            kb.mm([(bank[:], (wd if kc < 11 else wd2)[:, kc % 11, oo * 128:(oo + 1) * 128], zact[:, kc, :]) for kc in range(FCH)],
                                  R=zB + [wdB, wd2B], W=[bB])
                            xs = X[:, o, tt * TS:(tt + 1) * TS]
                            kb.op(dve, lambda: V_.scalar_tensor_tensor(out=xs, in0=bank[:], scalar=col(V_MOD[l] + 40, o), in1=xs, op0=ALU.mult, op1=ALU.add),
                                  R=[bB, vecB], W=[XB[o][tt]])
                        ws.release()
                        ws.release()
                    norm_tile2(xs_, xb_, TS, V_FN, None, lambda c: yn[:, c, :], ynB)
                    for sub in range(4):
                        s = oi % 2
                        oi += 1
                        for half in range(2):
                            bank, bB = nbank()
                            kb._pre(pe, ynB[half * 4:half * 4 + 4] + [identB], [bB])
                            ins = None
                            for j in range(4):
                                c = half * 4 + j
                                ins = P_.transpose(bank[:, j * 128:(j + 1) * 128], yn[:, c, sub * 128:(sub + 1) * 128], ident[:])
                            tok = pe.done(ins)
                            kb._post(tok, ynB[half * 4:half * 4 + 4] + [identB], [bB])
                            if half == 0:
                                kb.op(act, lambda: A_.activation(out=xst[:, s, 0:512], in_=bank[:], func=AF.Copy), R=[bB], W=[xstB[s]])
                            else:
                                kb.op(dve, lambda: V_.tensor_copy(out=xst[:, s, 512:1024], in_=bank[:]), R=[bB], W=[xstB[s]])
                        r0 = tt * TS + sub * 128
                        tk = kb.dma(sp, out_own[r0:r0 + 128, :], xst[:, s, :], R=[xstB[s]], sb=xstB[s])
                        out_toks.append(tk)
                for tk in out_toks:
                    sp.wait_tok(*tk)
                kb.barrier()
        else:
            kb.barrier()
        for e in (sp,):
            for b in xstB:
                e.wait_set(b.r)
                e.wait_set(b.w)
    return nc


def _t5_bucket(dist):
    dist = np.maximum(dist, 0)
    d = np.maximum(dist, 1).astype(np.float32)
    large = 16 + (np.log(d / 16) / np.float32(np.log(128 / 16)) * 16).astype(np.int32)
    large = np.minimum(large, 31)
    return np.where(dist < 16, dist, large)


def _tile_w(W, w):
    W = np.asarray(W, np.float32)
    K, N = W.shape
    return np.ascontiguousarray(W.reshape(K // 128, 128, N // w, w).transpose(2, 1, 0, 3).reshape(N // w, 128, (K // 128) * w))


def _tile_ffn(W):
    out = np.zeros((6, 128, 4096), np.float32)
    out[0:5] = _tile_w(W[:, 0:2560], 512)
    out[5, :, 0:2048] = _tile_w(W[:, 2560:2816], 256)[0]
    return out


def _fm(v, n):
    return np.ascontiguousarray(np.asarray(v, np.float32).reshape(n, 128).T)


_CACHE = {}


def _program(stop_after=99, debug=False):
    key = (stop_after, debug)
    if key not in _CACHE:
        _CACHE[key] = build_program(stop_after, debug)
    return _CACHE[key]


def make_in_maps(inp):
    f = lambda a: np.ascontiguousarray(np.asarray(a, dtype=np.float32))
    x = f(inp["x"]); c = f(inp["c"])
    rel_bias = f(inp["rel_bias"])
    shared = {
        "mod_w": np.stack([_tile_w(inp["mod_w"][l], 512) for l in range(2)]),
        "mod_b_T": np.ascontiguousarray(np.stack([_fm(inp["mod_b"][l], 48) for l in range(2)], axis=1)),
        "norm_mix_T": np.ascontiguousarray(np.stack([_fm(inp["norm_mix"][l], 8) for l in range(2)], axis=1)),
        "norm_ffn_T": np.ascontiguousarray(np.stack([_fm(inp["norm_ffn"][l], 8) for l in range(2)], axis=1)),
        "w_in": _tile_w(inp["lru_w_in"][0], 256),
        "conv_w_T": np.ascontiguousarray(np.stack([_fm(inp["lru_conv_w"][0][k], 10) for k in range(4)], axis=1)),
        "conv_b_T": _fm(inp["lru_conv_b"][0], 10),
        "w_gates": np.stack([_tile_w(inp["lru_w_gates"][0][hd], 512)[0] for hd in range(5)]),
        "b_gates_T": np.ascontiguousarray(np.stack([_fm(inp["lru_b_gates"][0][k], 10) for k in range(2)], axis=1)),
        "lambda_T": _fm(inp["lru_lambda"][0], 10),
        "w_out": _tile_w(inp["lru_w_out"][0], 256),
        "kv_mod_w": _tile_w(inp["kv_mod_w"], 512),
        "kv_mod_b_T": _fm(inp["kv_mod_b"], 16),
        "kv_norm_T": _fm(inp["kv_norm"], 8),
        "w_kv": _tile_w(inp["w_kv"], 512),
        "w_q": _tile_w(inp["attn_w_q"][0], 512),
        "w_o": _tile_w(inp["attn_w_o"][0], 512),
        "ffn_wg": np.stack([_tile_ffn(inp["ffn_w_gate"][l]) for l in range(2)]),
        "ffn_wu": np.stack([_tile_ffn(inp["ffn_w_up"][l]) for l in range(2)]),
        "ffn_wd": np.ascontiguousarray(np.stack([np.stack([_tile_w(inp["ffn_w_down"][l][hh * 1408:(hh + 1) * 1408], 256) for hh in range(2)], axis=1) for l in range(2)])),
        "final_norm_T": _fm(inp["final_norm"], 8),
        "bias31": np.ascontiguousarray(np.broadcast_to(rel_bias[:, 31][None, :], (128, 8))),
        "ident": np.eye(128, dtype=np.float32),
    }
    es = np.zeros((32, 32, 128), np.float32)
    for j in range(32):
        es[j, j, :] = 1.0
    shared["esel"] = es
    qi = np.arange(256)[None, :]
    ki = np.arange(256)[:, None]
    maps = []
    for core in range(8):
        b, k = divmod(core, 4)
        xb = x[b].reshape(32, 256, D)
        m = dict(shared)
        m["x_own"] = np.ascontiguousarray(xb[k::4].reshape(TOK, D))
        halo = np.zeros((8, 3, D), np.float32)
        hflag = np.ones((8, 3), np.float32)
        for i in range(8):
            gb = 4 * i + k
            if gb == 0:
                hflag[i] = 0.0
            else:
                halo[i] = xb[gb - 1][253:256]
        m["x_halo"] = np.ascontiguousarray(halo.reshape(24, NCH, 128).transpose(2, 1, 0))
        m["halo_flag"] = np.ascontiguousarray(np.broadcast_to(hflag.reshape(1, 24), (128, 24)))
        m["c_T"] = _fm(c[b], 8)
        sbt = np.empty((8, 5, 256, 256), np.float32)
        for s in range(5):
            dblk = (k - s) if s < 4 else (k + 1)
            dist = 256 * dblk + qi - ki
            idx = _t5_bucket(dist)
            g = rel_bias[:, idx]
            sbt[:, s] = np.where((dist >= 0)[None], g, np.float32(NEG))
        m["sbias"] = np.ascontiguousarray(sbt.reshape(8, 5, 2, 128, 256).transpose(0, 3, 1, 2, 4))
        vb = np.full((8, 32), -1e30, np.float32)
        of = np.zeros((8, 32), np.float32)
        for i in range(8):
            for r in range(4):
                for ip in range(8):
                    if ip < i or (ip == i and r < k):
                        vb[i, r * 8 + ip] = 0.0
            of[i, k * 8 + i] = 1.0
        m["vbias"] = np.ascontiguousarray(np.broadcast_to(vb[None], (128, 8, 32)))
        m["ownflag"] = np.ascontiguousarray(np.broadcast_to(of[None], (128, 8, 32)))
        rf = np.zeros((4,), np.float32); rf[k] = 1.0
        m["rankflag"] = np.ascontiguousarray(np.broadcast_to(rf[None], (128, 4)))
        maps.append(m)
    return maps


def kernel(**inputs):
    nc = _program()
    maps = make_in_maps(inputs)
    res = run_bass_kernel_spmd(nc, maps, core_ids=list(range(8)))
    out = np.empty((2, 32, 256, D), np.float32)
    for core in range(8):
        b, k = divmod(core, 4)
        out[b, k::4] = np.asarray(res.results[core]["out_own"], np.float32).reshape(8, 256, D)
    return out.reshape(2, 8192, D)
```

```python
import contextlib
import numpy as np
import ml_dtypes
import concourse.bass as bass
import concourse.mybir as mybir
from concourse.bass_utils import run_bass_kernel_spmd

F32 = mybir.dt.float32
BF16 = mybir.dt.bfloat16
AF = mybir.ActivationFunctionType
ALU = mybir.AluOpType
AX = mybir.AxisListType

D = 1024
NCH = 8
TOK = 2048
NT = 4
TS = 512
LW = 1280
LCH = 10
DFF = 2816
FCH = 22
NBLK = 8
BLK = 256
EPS = 1e-6
NEG = -30000.0
SLOT = 4096
NSLOT = 6


class Eng:
    def __init__(self, nc, eng, sem):
        self.e = eng
        self.sem = sem
        self.n = 0
        self.seen = {}
        self.skip_own = False

    def wait_tok(self, sem, val):
        if self.skip_own and sem is self.sem:
            return
        key = id(sem)
        if self.seen.get(key, (None, 0))[1] >= val:
            return
        self.seen[key] = (sem, val)
        self.e.wait_ge(sem, val)

    def wait_set(self, d):
        for sem, val in d.values():
            self.wait_tok(sem, val)

    def done(self, ins):
        self.n += 1
        ins.then_inc(self.sem, 1)
        return (self.sem, self.n)


class Buf:
    __slots__ = ("w", "r", "sem", "semv")

    def __init__(self, sem=None):
        self.w = {}
        self.r = {}
        self.sem = sem
        self.semv = 0


def _add(d, tok):
    sem, val = tok
    k = id(sem)
    if k not in d or d[k][1] < val:
        d[k] = (sem, val)


class KB:
    def __init__(self, nc, es):
        self.nc = nc
        self.es = es
        mk = lambda n: es.enter_context(nc.semaphore(n))
        self.pe = Eng(nc, nc.tensor, mk("s_pe"))
        self.pe.skip_own = True
        self.act = Eng(nc, nc.scalar, mk("s_act"))
        self.dve = Eng(nc, nc.vector, mk("s_dve"))
        self.pool = Eng(nc, nc.gpsimd, mk("s_pool"))
        self.sp = Eng(nc, nc.sync, mk("s_sp"))
        self.nsem = 0

    def dsem(self):
        self.nsem += 1
        return self.es.enter_context(self.nc.semaphore("d%d" % self.nsem))

    def dbuf(self):
        return Buf(self.dsem())

    def _pre(self, eng, R, W):
        for b in R:
            eng.wait_set(b.w)
        for b in W:
            eng.wait_set(b.w)
            eng.wait_set(b.r)

    def _post(self, tok, R, W):
        for b in R:
            _add(b.r, tok)
        for b in W:
            b.r = {}
            _add(b.w, tok)

    def op(self, eng, fn, R=(), W=()):
        self._pre(eng, R, W)
        tok = eng.done(fn())
        self._post(tok, R, W)
        return tok

    def mm(self, mms, R=(), W=()):
        eng = self.pe
        self._pre(eng, R, W)
        n = len(mms)
        ins = None
        for i, (o, l, r) in enumerate(mms):
            ins = self.nc.tensor.matmul(o, l, r, start=(i == 0), stop=(i == n - 1))
        tok = eng.done(ins)
        self._post(tok, R, W)
        return tok

    def dma(self, q, out, in_, R=(), W=(), sb=None):
        if sb is None:
            sb = W[0] if (W and W[0].sem is not None) else R[0]
        self._pre(q, R, W)
        ins = q.e.dma_start(out=out, in_=in_)
        sb.semv += 16
        ins.then_inc(sb.sem, 16)
        tok = (sb.sem, sb.semv)
        self._post(tok, R, W)
        return tok

    def barrier(self, bufs_tokens=()):
        engs = [self.pe, self.act, self.dve, self.pool, self.sp]
        for e in engs:
            for o in engs:
                if o is not e and o.n > 0:
                    e.wait_tok(o.sem, o.n)
        for e in engs:
            for (sem, val) in bufs_tokens:
                e.wait_tok(sem, val)


class WStream:
    def __init__(self, kb, ring, nslot):
        self.kb = kb
        self.ring = ring
        self.nslot = nslot
        self.bufs = [kb.dbuf() for _ in range(nslot)]
        self.units = []
        self.issued = 0
        self.taken = 0

    def plan(self, units):
        self.units.extend(units)
        self._fill()

    def _view(self, s, a, b):
        return self.ring[:, s, 0:a * b].rearrange("p (a b) -> p a b", a=a)

    def _fill(self):
        while self.issued < len(self.units) and self.issued < self.taken + self.nslot:
            src, a, b = self.units[self.issued]
            s = self.issued % self.nslot
            self.kb.dma(self.kb.pool, self._view(s, a, b), src, W=[self.bufs[s]])
            self.issued += 1

    def get(self, k=0):
        i = self.taken + k
        assert i < self.issued, "weight stream underflow"
        src, a, b = self.units[i]
        s = i % self.nslot
        return self._view(s, a, b), self.bufs[s]

    def release(self):
        self.taken += 1
        self._fill()


def build_program(stop_after=99, debug=False, no_cc=False):
    nc = bass.Bass("TRN2", target_bir_lowering=False)

    def din(name, shape, dt=F32):
        return nc.dram_tensor(name, list(shape), dt, kind="ExternalInput").ap()

    x_own = din("x_own", [TOK, D])
    x_halo = din("x_halo", [128, NCH, 24])
    halo_flag = din("halo_flag", [128, 24])
    c_T = din("c_T", [128, NCH])
    mod_w = din("mod_w", [2, 12, 128, 4096])
    mod_b_T = din("mod_b_T", [128, 2, 48])
    norm_mix_T = din("norm_mix_T", [128, 2, NCH])
    norm_ffn_T = din("norm_ffn_T", [128, 2, NCH])
    w_in = din("w_in", [10, 128, 2048])
    conv_w_T = din("conv_w_T", [128, 4, LCH])
    conv_b_T = din("conv_b_T", [128, LCH])
    w_gates = din("w_gates", [5, 128, 1024])
    b_gates_T = din("b_gates_T", [128, 2, LCH])
    lambda_T = din("lambda_T", [128, LCH])
    w_out = din("w_out", [4, 128, 2560])
    kv_mod_w = din("kv_mod_w", [4, 128, 4096])
    kv_mod_b_T = din("kv_mod_b_T", [128, 16])
    kv_norm_T = din("kv_norm_T", [128, NCH])
    w_kv = din("w_kv", [2, 128, 4096])
    w_q = din("w_q", [2, 128, 4096])
    w_o = din("w_o", [2, 128, 4096])
    ffn_wg = din("ffn_wg", [2, 6, 128, 4096])
    ffn_wu = din("ffn_wu", [2, 6, 128, 4096])
    ffn_wd = din("ffn_wd", [2, 4, 2, 128, 2816])
    final_norm_T = din("final_norm_T", [128, NCH])
    bias31 = din("bias31", [128, 8])
    sbias = din("sbias", [8, 128, 5, 2, 256])
    vbias = din("vbias", [128, NBLK, 32])
    ownflag = din("ownflag", [128, NBLK, 32])
    rankflag = din("rankflag", [128, 4])
    ident_d = din("ident", [128, 128])
    esel_d = din("esel", [32, 32, 128])
    out_own = nc.dram_tensor("out_own", [TOK, D], F32, kind="ExternalOutput").ap()
    if debug:
        dbg_out = nc.dram_tensor("dbg", [128, NCH, TOK], F32, kind="ExternalOutput").ap()

    st_in = nc.dram_tensor("st_in", [128, 160], F32)
    st_out = nc.dram_tensor("st_out", [4 * 128, 160], F32)
    kvi = [[nc.dram_tensor("kvi%d_%d" % (t_, q), [256, 512], BF16) for q in range(4)] for t_ in range(NT)]
    kvo = [[nc.dram_tensor("kvo%d_%d" % (t_, q), [4 * 256, 512], BF16) for q in range(4)] for t_ in range(NT)]
    km_in = nc.dram_tensor("km_in", [512, 8], F32)
    km_out = nc.dram_tensor("km_out", [4 * 512, 8], F32)

    with contextlib.ExitStack() as es:
        kb = KB(nc, es)
        pe, act, dve, pool, sp = kb.pe, kb.act, kb.dve, kb.pool, kb.sp
        V_, A_, P_, G_ = nc.vector, nc.scalar, nc.tensor, nc.gpsimd
        cc_sems = [es.enter_context(nc.semaphore("cc%d" % i)) for i in range(1)]
        cc_kv = [[es.enter_context(nc.semaphore("cck%d_%d" % (t_, q))) for q in range(4)] for t_ in range(NT)]

        def sb(stack, name, shape, dt):
            return stack.enter_context(nc.sbuf_tensor(name, list(shape), dt))

        X = sb(es, "X", [128, NCH, TOK], F32)
        XB = [[Buf() for _ in range(NT)] for _ in range(NCH)]
        def mk_ws(stack, nslot, tag):
            ring_ = sb(stack, "ring" + tag, [128, nslot, SLOT], BF16)
            return WStream(kb, ring_, nslot)

        ident = sb(es, "identf", [128, 128], F32)
        identb = sb(es, "identb", [128, 128], BF16)
        onesb = sb(es, "onesb", [128, 128], BF16)
        cst = sb(es, "cst", [128, 8], F32)
        vec = sb(es, "vec", [128, 512], F32)
        vecB = kb.dbuf()
        cB = Buf()
        V_CS = 0
        V_MOD = [8, 56]
        V_KVM = 104
        V_NM = 120
        V_NF = 136
        V_KVN = 152
        V_FN = 160
        V_MODB = 168
        V_KVB = 264
        V_GS = 280
        V_CW = 320
        V_CB = 360
        V_BG = 370
        V_LAM = 390
        V_B31 = 400
        V_RF = 408
        V_HF = 412
        csb = sb(es, "csb", [128, NCH, 1], BF16)
        hsem_bufs = []

        psf = [es.enter_context(nc.psum_tensor("psf%d" % i, [128, 512], F32)) for i in range(7)]
        psb = es.enter_context(nc.psum_tensor("psb", [128, 1024], BF16))
        PB = [Buf() for _ in range(7)]
        PBb = Buf()
        rot = {"i": 0}

        def nbank(lst=(0, 1, 2, 3, 4, 5, 6)):
            i = lst[rot["i"] % len(lst)]
            rot["i"] += 1
            return psf[i], PB[i]

        s0 = contextlib.ExitStack()
        s0.__enter__()
        ws = mk_ws(s0, 4, "0")
        xst = sb(s0, "xst0", [128, 2, D], F32)
        xstB = [kb.dbuf(), kb.dbuf()]
        def ld(dst_cols, src, n):
            kb.dma(sp, vec[:, dst_cols:dst_cols + n], src, W=[vecB])

        ld(V_CS, c_T[:, :], 8)
        ld(V_NM, norm_mix_T.rearrange("p a b -> p (a b)"), 16)
        ld(V_NF, norm_ffn_T.rearrange("p a b -> p (a b)"), 16)
        ld(V_KVN, kv_norm_T[:, :], 8)
        ld(V_FN, final_norm_T[:, :], 8)
        ld(V_MODB, mod_b_T.rearrange("p a b -> p (a b)"), 96)
        ld(V_KVB, kv_mod_b_T[:, :], 16)
        ld(V_CW, conv_w_T.rearrange("p a b -> p (a b)"), 40)
        ld(V_CB, conv_b_T[:, :], 10)
        ld(V_BG, b_gates_T.rearrange("p a b -> p (a b)"), 20)
        ld(V_LAM, lambda_T[:, :], 10)
        ld(V_B31, bias31[:, :], 8)
        ld(V_RF, rankflag[:, :], 4)
        ld(V_HF, halo_flag[:, :], 24)
        identB = kb.dbuf()
        kb.dma(sp, ident[:], ident_d[:, :], W=[identB])
        kb.op(pool, lambda: G_.memset(cst[:, 0:1], EPS), W=[cB])
        kb.op(pool, lambda: G_.memset(cst[:, 1:2], 1.0), W=[cB])
        kb.op(pool, lambda: G_.memset(cst[:, 2:3], 0.0), W=[cB])
        kb.op(pool, lambda: G_.memset(onesb[:], 1.0 / D), W=[cB])
        kb.op(dve, lambda: V_.tensor_copy(out=identb[:], in_=ident[:]), R=[identB], W=[cB])
        kb.op(act, lambda: A_.activation(out=vec[:, V_CS:V_CS + 8], in_=vec[:, V_CS:V_CS + 8], func=AF.Silu), W=[vecB])
        kb.op(dve, lambda: V_.tensor_copy(out=csb[:, :, 0], in_=vec[:, V_CS:V_CS + 8]), R=[vecB], W=[cB])
        kb.op(act, lambda: A_.activation(out=vec[:, V_LAM:V_LAM + 10], in_=vec[:, V_LAM:V_LAM + 10], func=AF.Exp, scale=-1.0), W=[vecB])
        kb.op(act, lambda: A_.activation(out=vec[:, V_LAM:V_LAM + 10], in_=vec[:, V_LAM:V_LAM + 10], func=AF.Ln, bias=cst[:, 1:2]), R=[cB], W=[vecB])
        kb.op(dve, lambda: V_.tensor_scalar(out=vec[:, V_LAM:V_LAM + 10], in0=vec[:, V_LAM:V_LAM + 10], scalar1=-8.0, scalar2=None, op0=ALU.mult), W=[vecB])

        def mod_units(wap, ncols):
            return [(wap[u].rearrange("p (a b) -> p a b", a=8), 8, 512) for u in range(ncols // 512)]

        vecB2 = Buf()

        def mod_unit_work(u, dst_col, bias_col, late):
            wv, wB = ws.get()
            bank, bB = nbank()
            for m in range(4):
                kb.mm([(bank[:, m:m + 1], wv[:, kc, m * 128:(m + 1) * 128], csb[:, kc, :]) for kc in range(NCH)], R=[wB, cB], W=[bB])
            ws.release()
            j0 = u * 4
            kb.op(dve, lambda: V_.tensor_tensor(out=vec[:, dst_col + j0:dst_col + j0 + 4], in0=bank[:, 0:4],
                                                in1=vec[:, bias_col + j0:bias_col + j0 + 4], op=ALU.add),
                  R=[bB, vecB], W=[vecB2 if late else vecB])

        ws.plan([(mod_w[0][u].rearrange("p (a b) -> p a b", a=8), 8, 512) for u in range(4)])
        for u in range(4):
            mod_unit_work(u, V_MOD[0], V_MODB, False)
        deferred = [(mod_w[0][u], u, V_MOD[0], V_MODB) for u in range(4, 12)]
        deferred += [(mod_w[1][u], u, V_MOD[1], V_MODB + 48) for u in range(12)]
        deferred += [(kv_mod_w[u], u, V_KVM, V_KVB) for u in range(4)]

        def deferred_for(tt, hd):
            sidx = tt * 5 + hd
            lo = min(2 * sidx, 4 + sidx)
            hi = min(2 * (sidx + 1), 4 + sidx + 1)
            return deferred[lo:min(hi, len(deferred))]

        def mk_gs(dst, normcol, sccol, late):
            kb.op(dve, lambda: V_.scalar_tensor_tensor(out=vec[:, dst:dst + 8], in0=vec[:, sccol:sccol + 8], scalar=1.0,
                                                       in1=vec[:, normcol:normcol + 8], op0=ALU.add, op1=ALU.mult),
                  R=[vecB2] if late else [], W=[vecB])

        mk_gs(V_GS + 0, V_NM + 0, V_MOD[0] + 8, False)

        def col(c0, j):
            return vec[:, c0 + j:c0 + j + 1]

        for t16 in range(TOK // 128):
            s = t16 % 2
            kb.dma(sp, xst[:, s, :], x_own[t16 * 128:(t16 + 1) * 128, :], W=[xstB[s]])
            tt = t16 // 4
            for half in range(2):
                bank, bB = nbank()
                kb._pre(pe, [xstB[s], identB], [bB])
                ins = None
                for j in range(4):
                    c = half * 4 + j
                    ins = P_.transpose(bank[:, j * 128:(j + 1) * 128], xst[:, s, c * 128:(c + 1) * 128], ident[:])
                tok = pe.done(ins)
                kb._post(tok, [xstB[s], identB], [bB])
                dst = X[:, half * 4:half * 4 + 4, t16 * 128:(t16 + 1) * 128]
                srcv = bank[:].rearrange("p (a b) -> p a b", a=4)
                Wb = [XB[half * 4 + j][tt] for j in range(4)]
                if half == 0:
                    kb.op(act, lambda: A_.activation(out=dst, in_=srcv, func=AF.Copy), R=[bB], W=Wb)
                else:
                    kb.op(dve, lambda: V_.tensor_copy(out=dst, in_=srcv), R=[bB], W=Wb)

        kb.barrier([t_ for b_ in xstB for t_ in b_.w.values()])
        s0.__exit__(None, None, None)
        with contextlib.ExitStack() as sa:
            ws = mk_ws(sa, 4, "A")
            h = sb(sa, "h", [128, NCH, TS], BF16)
            hB = [Buf() for _ in range(NCH)]
            sq, sqB = h, hB
            rstd = sb(sa, "rstd", [128, TS], F32)
            rstdB = Buf()
            tmpn = sb(sa, "tmpn", [128, 2, TS], F32)
            tmpB = [Buf(), Buf()]
            zact = sb(sa, "zact", [128, FCH, TS], BF16)
            zB = [Buf() for _ in range(FCH)]
            hh = sb(sa, "hh", [128, NCH, 24], BF16)
            hhB = Buf()
            xh = sb(sa, "xh", [128, NCH, 24], F32)
            xhB = kb.dbuf()
            xbrh = sb(sa, "xbrh", [128, LCH, 24], F32)
            xbrhB = Buf()
            xbr = sb(sa, "xbr", [128, 2, 2, 259], F32)
            xbrB = [Buf(), Buf()]
            xc = sb(sa, "xc", [128, 2, TS], F32)
            xcB = [Buf(), Buf()]
            xcb = sb(sa, "xcb", [128, 2, TS], BF16)
            xcbB = [Buf(), Buf()]
            tmp5 = sb(sa, "tmp5", [128, 2, 5, TS], F32)
            tmp5B = [[Buf() for _ in range(5)] for _ in range(2)]
            stt_ = sb(sa, "stt", [128, 2, NBLK, LCH], F32)
            sttB = kb.dbuf()
            stall = sb(sa, "stall", [128, 4, 2, NBLK, LCH], F32)
            stallB = kb.dbuf()
            Hst = sb(sa, "Hst", [128, NBLK, LCH], F32)
            HstB = Buf()
            Hrun = sb(sa, "Hrun", [128, LCH], F32)
            HrunB = Buf()
            Aall = sb(sa, "Aall", [128, 4, NBLK, LCH], F32)
            AallB = Buf()
            kmean = sb(sa, "kmean", [128, 4, NBLK], F32)
            kmB = kb.dbuf()
            ksb = zact[:, 0:4, :]
            vsb = zact[:, 4:8, :]

            kb.op(pool, lambda: G_.memset(stt_[:], 0.0), W=[sttB])
            kb.op(pool, lambda: G_.memset(Hst[:], 0.0), W=[HstB])
            kb.op(pool, lambda: G_.memset(Hrun[:], 0.0), W=[HrunB])

            def norm_tile(xsrc, xbufs, n, gs_col, sh_col, dst, dstB, out_dt_is_bf=True):
                for c in range(NCH):
                    if c % 2 == 0:
                        kb.op(act, lambda: A_.activation(out=sq[:, c, 0:n], in_=xsrc(c), func=AF.Square), R=[xbufs[c]], W=[sqB[c]])
                    else:
                        kb.op(dve, lambda: V_.tensor_tensor(out=sq[:, c, 0:n], in0=xsrc(c), in1=xsrc(c), op=ALU.mult), R=[xbufs[c]], W=[sqB[c]])
                bank, bB = nbank()
                kb.mm([(bank[:, 0:n], onesb[:], sq[:, c, 0:n]) for c in range(NCH)], R=sqB + [cB], W=[bB])
                kb.op(act, lambda: A_.activation(out=rstd[:, 0:n], in_=bank[:, 0:n], func=AF.Sqrt, bias=cst[:, 0:1]), R=[bB, cB], W=[rstdB])
                kb.op(dve, lambda: V_.reciprocal(out=rstd[:, 0:n], in_=rstd[:, 0:n]), W=[rstdB])
                for c in range(NCH):
                    tb = c % 2
                    if sh_col is None:
                        kb.op(dve, lambda: V_.scalar_tensor_tensor(out=dst(c), in0=xsrc(c), scalar=col(gs_col, c), in1=rstd[:, 0:n],
                                                                   op0=ALU.mult, op1=ALU.mult), R=[xbufs[c], rstdB, vecB], W=[dstB[c]])
                    else:
                        kb.op(dve, lambda: V_.scalar_tensor_tensor(out=tmpn[:, tb, 0:n], in0=xsrc(c), scalar=col(gs_col, c), in1=rstd[:, 0:n],
                                                                   op0=ALU.mult, op1=ALU.mult), R=[xbufs[c], rstdB, vecB], W=[tmpB[tb]])
                        if c % 2 == 0:
                            kb.op(act, lambda: A_.activation(out=dst(c), in_=tmpn[:, tb, 0:n], func=AF.Identity, bias=col(sh_col, c)),
                                  R=[tmpB[tb], vecB], W=[dstB[c]])
                        else:
                            kb.op(act, lambda: A_.activation(out=dst(c), in_=tmpn[:, tb, 0:n], func=AF.Identity, bias=col(sh_col, c)),
                                  R=[tmpB[tb], vecB], W=[dstB[c]])

            def xtile(tt):
                return (lambda c: X[:, c, tt * TS:(tt + 1) * TS]), [XB[c][tt] for c in range(NCH)]

            kb.dma(sp, xh[:], x_halo[:, :, :], W=[xhB])
            norm_tile(lambda c: xh[:, c, :], [xhB] * NCH, 24, V_GS + 0, V_MOD[0] + 0, lambda c: hh[:, c, :], [hhB] * NCH)

            CW = lambda k, ch: col(V_CW + k * LCH, ch)

            zf = zact[:, 10:19, :].rearrange("p a b -> p (a b)").bitcast(F32)
            xbr_s = [xbr, zf[:, 0:1036].rearrange("p (c s n) -> p c s n", c=2, s=2)]
            xc_s = [xc, zf[:, 1036:2060].rearrange("p (c n) -> p c n", c=2)]
            xcb_s = [xcb, zact[:, 19:21, :]]
            xbrB_s = [xbrB, [Buf(), Buf()]]
            xcB_s = [xcB, [Buf(), Buf()]]
            xcbB_s = [xcbB, [Buf(), Buf()]]
            alias_bufs = xbrB_s[1] + xcB_s[1] + xcbB_s[1]

            def lru_tile(tt, full):
                xs_, xb_ = xtile(tt)
                norm_tile(xs_, xb_, TS, V_GS + 0, V_MOD[0] + 0, lambda c: h[:, c, :], hB)
                def g_units(hd):
                    u_ = [(w_gates[hd].rearrange("p (a b) -> p a b", a=2), 2, 512)]
                    if full:
                        u_.append((w_in[hd].rearrange("p (a b) -> p a b", a=8), 8, 256))
                    else:
                        for (wap_, u2_, dc_, bc_) in deferred_for(tt, hd):
                            u_.append((wap_.rearrange("p (a b) -> p a b", a=8), 8, 512))
                    return u_
                x_unit = lambda hd: (w_in[5 + hd].rearrange("p (a b) -> p a b", a=8), 8, 256)
                units = [x_unit(0)]
                for hd in range(5):
                    if hd + 1 < 5:
                        units.append(x_unit(hd + 1))
                    units += g_units(hd)
                if full:
                    for o2 in range(4):
                        units.append((w_out[o2].rearrange("p (a b) -> p a b", a=LCH), LCH, 256))
                ws.plan(units)
                for e_ in (pe, act, dve, pool):
                    for b_ in zB[10:21]:
                        e_.wait_set(b_.w)
                        e_.wait_set(b_.r)

                def X_part(hd):
                    s_ = hd % 2
                    xbr_, xc_, xcb_ = xbr_s[s_], xc_s[s_], xcb_s[s_]
                    xbrB_, xcB_, xcbB_ = xbrB_s[s_], xcB_s[s_], xcbB_s[s_]
                    wx, wxB = ws.get()
                    for cc in range(2):
                        ch = 2 * hd + cc
                        bank, bB = nbank()
                        kb.mm([(bank[:], wx[:, kc, cc * 128:(cc + 1) * 128], h[:, kc, :]) for kc in range(NCH)], R=hB + [wxB], W=[bB])
                        if tt == 0 and not full:
                            bank2, bB2 = nbank()
                            kb.mm([(bank2[:, 0:24], wx[:, kc, cc * 128:(cc + 1) * 128], hh[:, kc, :]) for kc in range(NCH)], R=[hhB, wxB], W=[bB2])
                            kb.op(dve, lambda: V_.tensor_tensor(out=xbrh[:, ch, :], in0=bank2[:, 0:24], in1=vec[:, V_HF:V_HF + 24], op=ALU.mult),
                                  R=[bB2, vecB], W=[xbrhB])
                        kb.op(act, lambda: A_.activation(out=xbr_[:, cc, :, 3:259], in_=bank[:].rearrange("p (s n) -> p s n", s=2), func=AF.Copy),
                              R=[bB], W=[xbrB_[cc]])
                        kb.op(pool, lambda: G_.tensor_copy(out=xbr_[:, cc, :, 0:3],
                                                           in_=xbrh[:, ch, 6 * tt:6 * tt + 6].rearrange("p (s n) -> p s n", s=2)),
                              R=[xbrhB], W=[xbrB_[cc]])
                        xcv = xc_[:, cc, :].rearrange("p (s n) -> p s n", s=2)
                        kb.op(act, lambda: A_.activation(out=xcv, in_=xbr_[:, cc, :, 3:259], func=AF.Identity, scale=CW(3, ch), bias=col(V_CB, ch)),
                              R=[xbrB_[cc], vecB], W=[xcB_[cc]])
                        for k in range(3):
                            kb.op(dve, lambda: V_.scalar_tensor_tensor(out=xcv, in0=xbr_[:, cc, :, k:k + 256], scalar=CW(k, ch), in1=xcv,
                                                                       op0=ALU.mult, op1=ALU.add), R=[xbrB_[cc], vecB], W=[xcB_[cc]])
                        kb.op(dve, lambda: V_.tensor_copy(out=xcb_[:, cc, :], in_=xc_[:, cc, :]), R=[xcB_[cc]], W=[xcbB_[cc]])
                    ws.release()

                def G_part(hd):
                    s_ = hd % 2
                    xc_, xcb_ = xc_s[s_], xcb_s[s_]
                    xcB_, xcbB_ = xcB_s[s_], xcbB_s[s_]
                    wg, wgB = ws.get()
                    gate_banks = []
                    for cc in range(2):
                        bank_r, bBr = nbank()
                        kb.mm([(bank_r[:], wg[:, kc, cc * 128:(cc + 1) * 128], xcb_[:, kc, :]) for kc in range(2)], R=xcbB_ + [wgB], W=[bBr])
                        bank_i, bBi = nbank()
                        kb.mm([(bank_i[:], wg[:, kc, 256 + cc * 128:256 + (cc + 1) * 128], xcb_[:, kc, :]) for kc in range(2)], R=xcbB_ + [wgB], W=[bBi])
                        gate_banks.append((bank_r, bBr, bank_i, bBi))
                    ws.release()
                    if full:
                        wy, wyB = ws.get()
                    T = lambda cc, q: tmp5[:, cc, q, :]
                    TB = lambda cc, q: tmp5B[cc][q]
                    for cc in range(2):
                        ch = 2 * hd + cc
                        bank_r, bBr, bank_i, bBi = gate_banks[cc]
                        if full:
                            kb.op(act, lambda: A_.activation(out=T(cc, 0), in_=bank_r[:], func=AF.Sigmoid, bias=col(V_BG, ch)), R=[bBr, vecB], W=[TB(cc, 0)])
                        else:
                            for seg in range(2):
                                blk = 2 * tt + seg
                                sl = slice(seg * 256, (seg + 1) * 256)
                                kb.op(act, lambda: A_.activation(out=T(cc, 0)[:, sl], in_=bank_r[:, sl], func=AF.Sigmoid, bias=col(V_BG, ch),
                                                                 accum_out=stt_[:, 0, blk, ch:ch + 1]), R=[bBr, vecB], W=[TB(cc, 0), sttB])
                        kb.op(act, lambda: A_.activation(out=T(cc, 1), in_=bank_i[:], func=AF.Sigmoid, bias=col(V_BG + LCH, ch)), R=[bBi, vecB], W=[TB(cc, 1)])
                    for cc in range(2):
                        ch = 2 * hd + cc
                        kb.op(act, lambda: A_.activation(out=T(cc, 2), in_=T(cc, 0), func=AF.Exp, scale=col(V_LAM, ch)), R=[TB(cc, 0), vecB], W=[TB(cc, 2)])
                        kb.op(dve, lambda: V_.tensor_tensor(out=T(cc, 0), in0=T(cc, 2), in1=T(cc, 2), op=ALU.mult), R=[TB(cc, 2)], W=[TB(cc, 0)])
                        kb.op(dve, lambda: V_.tensor_tensor(out=T(cc, 3), in0=T(cc, 1), in1=xc_[:, cc, :], op=ALU.mult), R=[TB(cc, 1), xcB_[cc]], W=[TB(cc, 3)])
                    for cc in range(2):
                        kb.op(act, lambda: A_.activation(out=T(cc, 0), in_=T(cc, 0), func=AF.Sqrt, scale=-1.0, bias=cst[:, 1:2]), R=[cB], W=[TB(cc, 0)])
                    for cc in range(2):
                        ch = 2 * hd + cc
                        kb.op(dve, lambda: V_.tensor_tensor(out=T(cc, 3), in0=T(cc, 3), in1=T(cc, 0), op=ALU.mult), R=[TB(cc, 0)], W=[TB(cc, 3)])
                        for seg in range(2):
                            blk = 2 * tt + seg
                            sl = slice(seg * 256, (seg + 1) * 256)
                            if full:
                                kb.op(dve, lambda: V_.tensor_tensor_scan(out=T(cc, 4)[:, sl], data0=T(cc, 2)[:, sl], data1=T(cc, 3)[:, sl],
                                                                         initial=Hst[:, blk, ch:ch + 1], op0=ALU.mult, op1=ALU.add),
                                      R=[TB(cc, 2), TB(cc, 3), HstB], W=[TB(cc, 4)])
                            else:
                                kb.op(dve, lambda: V_.tensor_tensor_scan(out=T(cc, 4)[:, sl], data0=T(cc, 2)[:, sl], data1=T(cc, 3)[:, sl],
                                                                         initial=0.0, op0=ALU.mult, op1=ALU.add),
                                      R=[TB(cc, 2), TB(cc, 3)], W=[TB(cc, 4)])
                                kb.op(pool, lambda: G_.tensor_copy(out=stt_[:, 1, blk, ch:ch + 1], in_=T(cc, 4)[:, seg * 256 + 255:seg * 256 + 256]),
                                      R=[TB(cc, 4)], W=[sttB])
                    if not full:
                        for (wap_, u_, dc_, bc_) in deferred_for(tt, hd):
                            mod_unit_work(u_, dc_, bc_, True)
                    if full and hd == 1 and cc_state['pending'] is not None:
                        issue_kv_cc(cc_state['pending'])
                        cc_state['pending'] = None
                    if full:
                        for cc in range(2):
                            ch = 2 * hd + cc
                            banky, bBy = nbank()
                            kb.mm([(banky[:], wy[:, kc, cc * 128:(cc + 1) * 128], h[:, kc, :]) for kc in range(NCH)], R=hB + [wyB], W=[bBy])
                            kb.op(act, lambda: A_.activation(out=T(cc, 1), in_=banky[:], func=AF.Gelu_apprx_tanh), R=[bBy], W=[TB(cc, 1)])
                            kb.op(dve, lambda: V_.tensor_tensor(out=zact[:, ch, :], in0=T(cc, 4), in1=T(cc, 1), op=ALU.mult), R=[TB(cc, 4), TB(cc, 1)], W=[zB[ch]])
                        ws.release()

                X_part(0)
                for hd in range(5):
                    if hd + 1 < 5:
                        X_part(hd + 1)
                    G_part(hd)
                for k_ in range(10, 21):
                    for b_ in alias_bufs:
                        for t_ in list(b_.w.values()) + list(b_.r.values()):
                            _add(zB[k_].r, t_)
                if full:
                    for o2 in range(4):
                        wo, woB = ws.get()
                        for oo in range(2):
                            o = o2 * 2 + oo
                            bank, bB = nbank()
                            kb.mm([(bank[:], wo[:, kc, oo * 128:(oo + 1) * 128], zact[:, kc, :]) for kc in range(LCH)], R=zB[0:LCH] + [woB], W=[bB])
                            xs = X[:, o, tt * TS:(tt + 1) * TS]
                            kb.op(dve, lambda: V_.scalar_tensor_tensor(out=xs, in0=bank[:], scalar=col(V_MOD[0] + 16, o), in1=xs, op0=ALU.mult, op1=ALU.add),
                                  R=[bB, vecB], W=[XB[o][tt]])
                        ws.release()

            for tt in range(NT if stop_after >= 1 else 0):
                lru_tile(tt, False)

            if stop_after >= 1:
                mk_gs(V_GS + 8, V_NF + 0, V_MOD[0] + 32, True)
                mk_gs(V_GS + 16, V_NM + 8, V_MOD[1] + 8, True)
                mk_gs(V_GS + 24, V_NF + 8, V_MOD[1] + 32, True)
                mk_gs(V_GS + 32, V_KVN, V_KVM + 8, True)
            stinB = kb.dbuf()
            kb.dma(pool, st_in.ap(), stt_[:].rearrange("p a b c -> p (a b c)"), R=[sttB], W=[stinB])
            kb._pre(pool, [stinB], [])
            if not no_cc or no_cc == 2:
                pool.e.collective_compute("AllGather", ALU.bypass, replica_groups=[[0, 1, 2, 3], [4, 5, 6, 7]],
                                          ins=[st_in.ap().opt()], outs=[st_out.ap().opt()]).then_inc(cc_sems[0])
                pool.e.wait_ge(cc_sems[0], 1)
            kb.dma(pool, stall[:].rearrange("p r a b c -> p r (a b c)"), st_out.ap().rearrange("(r p) f -> p r f", p=128), W=[stallB])
            for r in range(4):
                for i in range(NBLK):
                    kb.op(dve, lambda: V_.tensor_tensor(out=Aall[:, r, i, :], in0=stall[:, r, 0, i, :], in1=vec[:, V_LAM:V_LAM + LCH], op=ALU.mult),
                          R=[stallB, vecB], W=[AallB])
            kb.op(act, lambda: A_.activation(out=Aall[:], in_=Aall[:], func=AF.Exp), W=[AallB])
            for i in range(NBLK):
                for r in range(4):
                    kb.op(dve, lambda: V_.scalar_tensor_tensor(out=Hst[:, i, :], in0=Hrun[:], scalar=col(V_RF, r), in1=Hst[:, i, :],
                                                               op0=ALU.mult, op1=ALU.add), R=[HrunB, vecB], W=[HstB])
                    kb.op(dve, lambda: V_.tensor_tensor(out=Hrun[:], in0=Hrun[:], in1=Aall[:, r, i, :], op=ALU.mult), R=[AallB], W=[HrunB])
                    kb.op(dve, lambda: V_.tensor_tensor(out=Hrun[:], in0=Hrun[:], in1=stall[:, r, 1, i, :], op=ALU.add), R=[stallB], W=[HrunB])

            def ffn_tile(l, tt, h_, hB_, sq_, zact_, zB_, normf):
                xs_, xb_ = xtile(tt)
                normf(xs_, xb_, TS, V_GS + l * 16 + 8, V_MOD[l] + 24, lambda c: h_[:, c, :], hB_)
                units = []
                groups = [(g * 512, 512) for g in range(5)] + [(2560, 256)]
                for (c0, w) in groups:
                    units.append((ffn_wg[l, c0 // 512][:, 0:8 * w].rearrange("p (a b) -> p a b", a=8), 8, w))
                    units.append((ffn_wu[l, c0 // 512][:, 0:8 * w].rearrange("p (a b) -> p a b", a=8), 8, w))
                for og in range(4):
                    units.append((ffn_wd[l, og, 0].rearrange("p (a b) -> p a b", a=11), 11, 256))
                    units.append((ffn_wd[l, og, 1].rearrange("p (a b) -> p a b", a=11), 11, 256))
                ws.plan(units)
                for (c0, w) in groups:
                    wgt, wgtB = ws.get(0)
                    wup, wupB = ws.get(1)
                    for m in range(w // 128):
                        fc = c0 // 128 + m
                        bg, bgB = nbank()
                        kb.mm([(bg[:], wgt[:, kc, m * 128:(m + 1) * 128], h_[:, kc, :]) for kc in range(NCH)], R=hB_ + [wgtB], W=[bgB])
                        bu, buB = nbank()
                        kb.mm([(bu[:], wup[:, kc, m * 128:(m + 1) * 128], h_[:, kc, :]) for kc in range(NCH)], R=hB_ + [wupB], W=[buB])
                        tb = fc % 2
                        kb.op(act, lambda: A_.activation(out=tmpn[:, tb, :], in_=bg[:], func=AF.Silu), R=[bgB], W=[tmpB[tb]])
                        kb.op(dve, lambda: V_.tensor_tensor(out=zact_[:, fc, :], in0=tmpn[:, tb, :], in1=bu[:], op=ALU.mult), R=[tmpB[tb], buB], W=[zB_[fc]])
                    ws.release()
                    ws.release()
                for og in range(4):
                    wd, wdB = ws.get(0)
                    wd2, wd2B = ws.get(1)
                    for oo in range(2):
                        o = og * 2 + oo
                        bank, bB = nbank()
                        kb.mm([(bank[:], (wd if kc < 11 else wd2)[:, kc % 11, oo * 128:(oo + 1) * 128], zact_[:, kc, :]) for kc in range(FCH)],
                              R=zB_ + [wdB, wd2B], W=[bB])
                        xs = X[:, o, tt * TS:(tt + 1) * TS]
                        kb.op(dve, lambda: V_.scalar_tensor_tensor(out=xs, in0=bank[:], scalar=col(V_MOD[l] + 40, o), in1=xs, op0=ALU.mult, op1=ALU.add),
                              R=[bB, vecB], W=[XB[o][tt]])
                    ws.release()
                    ws.release()

            kvinB = [kb.dbuf() for _ in range(NT)]
            kvinVB = [kb.dbuf() for _ in range(NT)]
            cc_state = {'pending': None}

            def issue_kv_cc(t_):
                kb._pre(pool, [kvinB[t_], kvinVB[t_]], [])
                if not no_cc:
                    for q in range(4):
                        pool.e.collective_compute('AllGather', ALU.bypass, replica_groups=[[0, 1, 2, 3], [4, 5, 6, 7]],
                                                  ins=[kvi[t_][q].ap().opt()], outs=[kvo[t_][q].ap().opt()]).then_inc(cc_kv[t_][q])

            kminB = kb.dbuf()

            def kv_tile(tt):
                xs_, xb_ = xtile(tt)
                norm_tile(xs_, xb_, TS, V_GS + 32, V_KVM + 0, lambda c: h[:, c, :], hB)
                ws.plan([(w_kv[g].rearrange("p (a b) -> p a b", a=8), 8, 512) for g in range(2)])
                wk, wkB = ws.get()
                for hk in range(4):
                    bank, bB = nbank()
                    kb.mm([(bank[:], wk[:, kc, hk * 128:(hk + 1) * 128], h[:, kc, :]) for kc in range(NCH)], R=hB + [wkB], W=[bB])
                    kb.op(act, lambda: A_.activation(out=ksb[:, hk, :], in_=bank[:], func=AF.Copy), R=[bB], W=[zB[hk]])
                ws.release()
                for pr in range(2):
                    _add(kvinB[tt].w, kb.dma(sp, kvi[tt][pr].ap().rearrange("(h d) t -> d h t", d=128), ksb[:, 2 * pr:2 * pr + 2, :],
                                             R=[zB[2 * pr], zB[2 * pr + 1]], sb=kvinB[tt]))
                wv, wvB = ws.get()
                for sub in range(4):
                    bank, bB = nbank()
                    kb.mm([(bank[:], h[:, kc, sub * 128:(sub + 1) * 128], wv[:, kc, :]) for kc in range(NCH)], R=hB + [wvB], W=[bB])
                    if sub % 2 == 0:
                        kb.op(act, lambda: A_.activation(out=vsb[:, sub, :], in_=bank[:], func=AF.Copy), R=[bB], W=[zB[4 + sub]])
                    else:
                        kb.op(dve, lambda: V_.tensor_copy(out=vsb[:, sub, :], in_=bank[:]), R=[bB], W=[zB[4 + sub]])
                ws.release()
                for sub in range(4):
                    n_ = tt * 4 + sub
                    for pr in range(2):
                        dstv = kvi[tt][2 + pr].ap()[:, sub * 128:(sub + 1) * 128].rearrange("(h p) d -> p h d", p=128)
                        _add(kvinVB[tt].w, kb.dma(sp, dstv, vsb[:, sub, pr * 256:(pr + 1) * 256].rearrange("p (h d) -> p h d", d=128), R=[zB[4 + sub]], sb=kvinVB[tt]))

            for tt in range(NT if stop_after >= 2 else 0):
                lru_tile(tt, True)
                ffn_tile(0, tt, h, hB, sq, zact, zB, norm_tile)
                kv_tile(tt)
                cc_state['pending'] = tt
            if cc_state['pending'] is not None:
                issue_kv_cc(cc_state['pending'])
                cc_state['pending'] = None

            if debug and stop_after <= 2:
                dbB = kb.dbuf()
                tk = kb.dma(sp, dbg_out[:, :, :], X[:], R=[b for row in XB for b in row], W=[dbB])
                sp.wait_tok(*tk)

            kb.barrier([t_ for b_ in kvinB + kvinVB for t_ in b_.w.values()])
        kb.barrier()
        if stop_after >= 3:
            with contextlib.ExitStack() as sbk:
                QT = sb(sbk, "QT", [128, 8, TOK], BF16)
                QTB = [[Buf() for _ in range(NT)] for _ in range(8)]
                with contextlib.ExitStack() as s1:
                    ws = mk_ws(s1, 4, "Q")
                    h = sb(s1, "h1", [128, NCH, TS], BF16)
                    hB = [Buf() for _ in range(NCH)]
                    sq, sqB = h, hB
                    rstd = sb(s1, "rstd1", [128, TS], F32)
                    rstdB = Buf()
                    tmpn = sb(s1, "tmpn1", [128, 2, TS], F32)
                    tmpB = [Buf(), Buf()]

                    def norm_tile1(xsrc, xbufs, n, gs_col, sh_col, dst, dstB):
                        for c in range(NCH):
                            if c % 2 == 0:
                                kb.op(act, lambda: A_.activation(out=sq[:, c, 0:n], in_=xsrc(c), func=AF.Square), R=[xbufs[c]], W=[sqB[c]])
                            else:
                                kb.op(dve, lambda: V_.tensor_tensor(out=sq[:, c, 0:n], in0=xsrc(c), in1=xsrc(c), op=ALU.mult), R=[xbufs[c]], W=[sqB[c]])
                        bank, bB = nbank()
                        kb.mm([(bank[:, 0:n], onesb[:], sq[:, c, 0:n]) for c in range(NCH)], R=sqB + [cB], W=[bB])
                        kb.op(act, lambda: A_.activation(out=rstd[:, 0:n], in_=bank[:, 0:n], func=AF.Sqrt, bias=cst[:, 0:1]), R=[bB, cB], W=[rstdB])
                        kb.op(dve, lambda: V_.reciprocal(out=rstd[:, 0:n], in_=rstd[:, 0:n]), W=[rstdB])
                        for c in range(NCH):
                            tb = c % 2
                            kb.op(dve, lambda: V_.scalar_tensor_tensor(out=tmpn[:, tb, 0:n], in0=xsrc(c), scalar=col(gs_col, c), in1=rstd[:, 0:n],
                                                                       op0=ALU.mult, op1=ALU.mult), R=[xbufs[c], rstdB, vecB], W=[tmpB[tb]])
                            if c % 2 == 0:
                                kb.op(act, lambda: A_.activation(out=dst(c), in_=tmpn[:, tb, 0:n], func=AF.Identity, bias=col(sh_col, c)),
                                      R=[tmpB[tb], vecB], W=[dstB[c]])
                            else:
                                kb.op(act, lambda: A_.activation(out=dst(c), in_=tmpn[:, tb, 0:n], func=AF.Identity, bias=col(sh_col, c)),
                                      R=[tmpB[tb], vecB], W=[dstB[c]])

                    qscale = 128.0 ** -0.5
                    for tt in range(NT):
                        xs_ = lambda c, tt=tt: X[:, c, tt * TS:(tt + 1) * TS]
                        norm_tile1(xs_, [XB[c][tt] for c in range(NCH)], TS, V_GS + 16, V_MOD[1] + 0, lambda c: h[:, c, :], hB)
                        ws.plan([(w_q[g].rearrange("p (a b) -> p a b", a=8), 8, 512) for g in range(2)])
                        for g in range(2):
                            wq_, wqB = ws.get()
                            for m in range(4):
                                hq = g * 4 + m
                                bank, bB = nbank()
                                kb.mm([(bank[:], wq_[:, kc, m * 128:(m + 1) * 128], h[:, kc, :]) for kc in range(NCH)], R=hB + [wqB], W=[bB])
                                if m % 2 == 0:
                                    kb.op(act, lambda: A_.activation(out=QT[:, hq, tt * TS:(tt + 1) * TS], in_=bank[:], func=AF.Copy, scale=qscale),
                                          R=[bB], W=[QTB[hq][tt]])
                                else:
                                    kb.op(dve, lambda: V_.tensor_scalar(out=QT[:, hq, tt * TS:(tt + 1) * TS], in0=bank[:], scalar1=qscale, scalar2=None, op0=ALU.mult),
                                          R=[bB], W=[QTB[hq][tt]])
                            ws.release()
                    kb.barrier()
                kb.barrier()
                with contextlib.ExitStack() as s2:
                    KT = sb(s2, "KT", [128, 4, TOK], BF16)
                    KTB = [kb.dbuf() for _ in range(NT)]
                    VV = sb(s2, "VV", [128, 64, 130], BF16)
                    VVB = [[kb.dbuf() for _ in range(NT)] for _ in range(4)]
                    Otok = sb(s2, "Otok", [128, 16, 128], BF16)
                    OtokB = Buf()
                    PT = sb(s2, "PT", [128, 4, 2, 256], BF16)
                    PTB = [Buf() for _ in range(4)]
                    selall = sb(s2, "selall", [128, 16, 32], F32)
                    Oacc = sb(s2, "Oacc", [128, 2, 2, 130], F32)
                    OaccB = [Buf(), Buf()]
                    sbs = sb(s2, "sbs", [128, 5, 2, 256], BF16)
                    sbsB = kb.dbuf()
                    kmf = sb(s2, "kmf", [128, 4, 32], F32)
                    kmfB = kb.dbuf()
                    kmT = sb(s2, "kmT", [128, 4, 32], BF16)
                    kmTB = Buf()
                    vbs = sb(s2, "vbs", [128, NBLK, 32], F32)
                    ofs = sb(s2, "ofs", [128, NBLK, 32], F32)
                    vbB = kb.dbuf()
                    ofB = kb.dbuf()
                    gm = sb(s2, "gm", [128, 2, 32], F32); gmB = [Buf(), Buf()]
                    top8 = sb(s2, "top8", [128, 2, 8], F32); t8B = [Buf(), Buf()]
                    selB = [Buf() for _ in range(NBLK)]
                    rec = sb(s2, "rec", [128, 2], F32); recB = Buf()

                    kb.dma(sp, vbs[:], vbias[:, :, :], W=[vbB])
                    kb.dma(sp, ofs[:], ownflag[:, :, :], W=[ofB])
                    kb.op(pool, lambda: G_.memset(VV[:, :, 128:130], 1.0), W=[b_ for row_ in VVB for b_ in row_])

                    for hq in range(8):
                        kvh = hq // 2
                        if hq % 2 == 0:
                            for t_ in range(NT):
                                if not no_cc:
                                    sp.wait_tok(cc_kv[t_][kvh // 2], 1)
                                kb.dma(sp, KT[:, :, t_ * TS:(t_ + 1) * TS],
                                       kvo[t_][kvh // 2].ap().rearrange("(r hh d) t -> hh d r t", r=4, hh=2)[kvh % 2], W=[KTB[t_]])
                            kb.op(dve, lambda: V_.tensor_reduce(out=kmf[:, 0, 0:32], in_=KT[:].rearrange("p r (i n) -> p (r i) n", n=BLK),
                                                                axis=AX.X, op=ALU.add), R=KTB, W=[kmfB])
                            kb.op(dve, lambda: V_.tensor_scalar(out=kmT[:, kvh, :], in0=kmf[:, 0, 0:32], scalar1=1.0 / BLK, scalar2=None, op0=ALU.mult),
                                  R=[kmfB], W=[kmTB])
                            for t_ in range(NT):
                                if not no_cc:
                                    sp.wait_tok(cc_kv[t_][2 + kvh // 2], 1)
                                for r_ in range(4):
                                    r0_ = r_ * 256 + (kvh % 2) * 128
                                    kb.dma(sp, VV[:, r_ * 16 + t_ * 4:r_ * 16 + t_ * 4 + 4, 0:128],
                                           kvo[t_][2 + kvh // 2].ap()[r0_:r0_ + 128, :].rearrange("p (n d) -> p n d", d=128), W=[VVB[r_][t_]])
                        kb.dma(pool, sbs[:], sbias[hq], W=[sbsB])
                        def gate_qblock(i):
                            for qt in (2 * i, 2 * i + 1):
                                tt = qt // 4
                                g_ = qt % 2
                                bank, bB = nbank((0, 1, 2))
                                kb.mm([(bank[:, 0:32], QT[:, hq, qt * 128:(qt + 1) * 128], kmT[:, kvh, :])], R=[QTB[hq][tt], kmTB], W=[bB])
                                kb.op(dve, lambda: V_.tensor_tensor(out=gm[:, g_, :], in0=bank[:, 0:32], in1=vbs[:, i, :], op=ALU.add), R=[bB, vbB], W=[gmB[g_]])
                                kb.op(dve, lambda: V_.max(out=top8[:, g_, :], in_=gm[:, g_, :]), R=[gmB[g_]], W=[t8B[g_]])
                                kb.op(dve, lambda: V_.tensor_scalar(out=top8[:, g_, 2:3], in0=top8[:, g_, 2:3], scalar1=-1e29, scalar2=None, op0=ALU.max), W=[t8B[g_]])
                                kb.op(dve, lambda: V_.tensor_scalar(out=gm[:, g_, :], in0=gm[:, g_, :], scalar1=top8[:, g_, 2:3], scalar2=None, op0=ALU.is_ge),
                                      R=[t8B[g_]], W=[gmB[g_]])
                                kb.op(pool, lambda: G_.tensor_tensor(out=selall[:, qt, :], in0=gm[:, g_, :], in1=ofs[:, i, :], op=ALU.add),
                                      R=[gmB[g_], ofB], W=[selB[i]])

                        gate_qblock(0)
                        items = []
                        for i in range(NBLK):
                            for ip in range(i + 1):
                                for r in range(4):
                                    if ip == i:
                                        slot = r
                                    elif ip == i - 1 and r == 3:
                                        slot = 4
                                    else:
                                        slot = None
                                    items.append((i, r, ip, slot, ip == 0 and r == 0, ip == i and r == 3))
                        st_ = {}

                        def emit_s(n):
                            i, r, ip, slot, first, last = items[n]
                            tt = i // 2
                            if first and i + 1 < NBLK:
                                gate_qblock(i + 1)
                            sbank, sbB_ = nbank((0, 1, 2, 3))
                            sv = sbank[:].rearrange("p (u n) -> p u n", u=2)
                            Rl = [KTB[ip // 2], QTB[hq][tt]] + ([sbsB, cB] if slot is not None else [])
                            kb._pre(pe, Rl, [sbB_])
                            ins = None
                            for u in range(2):
                                ins = P_.matmul(sv[:, u, :], KT[:, r, ip * 256 + u * 128:ip * 256 + (u + 1) * 128], QT[:, hq, i * 256:(i + 1) * 256],
                                                start=True, stop=(slot is None))
                                if slot is not None:
                                    ins = P_.matmul(sv[:, u, :], identb[:], sbs[:, slot, u, :], start=False, stop=True)
                            tok = pe.done(ins)
                            kb._post(tok, Rl, [sbB_])
                            p = n % 4
                            if slot is None:
                                kb.op(act, lambda: A_.activation(out=PT[:, p, :, :], in_=sv, func=AF.Exp, bias=col(V_B31, hq)), R=[sbB_, vecB], W=[PTB[p]])
                            else:
                                kb.op(act, lambda: A_.activation(out=PT[:, p, :, :], in_=sv, func=AF.Exp), R=[sbB_], W=[PTB[p]])

                        def emit_pv(n):
                            i, r, ip, slot, first, last = items[n]
                            j = r * 8 + ip
                            p = n % 4
                            par = i % 2
                            pvb, pvB = nbank((4, 5, 6))
                            kb._pre(pe, [PTB[p], VVB[r][ip // 2]], [pvB])
                            ins = None
                            for v in range(2):
                                for u in range(2):
                                    ins = P_.matmul(pvb[:, v * 256:v * 256 + 129], PT[:, p, u, v * 128:(v + 1) * 128], VV[:, r * 16 + ip * 2 + u, 0:129],
                                                    start=(u == 0), stop=(u == 1))
                            tok = pe.done(ins)
                            kb._post(tok, [PTB[p], VVB[r][ip // 2]], [pvB])
                            if first:
                                kb.op(pool, lambda: G_.memset(Oacc[:, par, :, :], 0.0), W=[OaccB[par]])
                            for v in range(2):
                                kb.op(dve, lambda: V_.scalar_tensor_tensor(out=Oacc[:, par, v, 0:129], in0=pvb[:, v * 256:v * 256 + 129],
                                                                           scalar=selall[:, 2 * i + v, j:j + 1], in1=Oacc[:, par, v, 0:129],
                                                                           op0=ALU.mult, op1=ALU.add), R=[pvB, selB[i]], W=[OaccB[par]])
                            if last:
                                for v in range(2):
                                    kb.op(dve, lambda: V_.reciprocal(out=rec[:, v:v + 1], in_=Oacc[:, par, v, 128:129]), R=[OaccB[par]], W=[recB])
                                    kb.op(dve, lambda: V_.tensor_scalar(out=Otok[:, 2 * i + v, :], in0=Oacc[:, par, v, 0:128], scalar1=rec[:, v:v + 1],
                                                                        scalar2=None, op0=ALU.mult), R=[OaccB[par], recB], W=[OtokB])

                        for n in range(len(items) + 1):
                            if n < len(items):
                                emit_s(n)
                            if n >= 3:
                                emit_pv(n - 3)
                        emit_pv(len(items) - 2)
                        emit_pv(len(items) - 1)
                        for g4 in range(4):
                            kb._pre(pe, [OtokB, cB], [PBb])
                            ins = None
                            for j4 in range(4):
                                qt = g4 * 4 + j4
                                ins = P_.transpose(psb[:, j4 * 128:(j4 + 1) * 128], Otok[:, qt, :], identb[:])
                            tok = pe.done(ins)
                            kb._post(tok, [OtokB, cB], [PBb])
                            kb.op(act, lambda: A_.activation(out=QT[:, hq, g4 * 512:(g4 + 1) * 512], in_=psb[:, 0:512], func=AF.Copy), R=[PBb], W=[QTB[hq][g4]])
                    kb.barrier()
                kb.barrier()
                s3 = contextlib.ExitStack()
                s3.__enter__()
                ws = mk_ws(s3, 4, "O")
                for tt in range(NT):
                    ws.plan([(w_o[g].rearrange("p (a b) -> p a b", a=8), 8, 512) for g in range(2)])
                    for g in range(2):
                        wo_, woB = ws.get()
                        for m in range(4):
                            o = g * 4 + m
                            bank, bB = nbank()
                            kb.mm([(bank[:], wo_[:, kc, m * 128:(m + 1) * 128], QT[:, kc, tt * TS:(tt + 1) * TS]) for kc in range(8)],
                                  R=[QTB[kc][tt] for kc in range(8)] + [woB], W=[bB])
                            xs = X[:, o, tt * TS:(tt + 1) * TS]
                            kb.op(dve, lambda: V_.scalar_tensor_tensor(out=xs, in0=bank[:], scalar=col(V_MOD[1] + 16, o), in1=xs, op0=ALU.mult, op1=ALU.add),
                                  R=[bB, vecB], W=[XB[o][tt]])
                        ws.release()
                kb.barrier()
                s3.__exit__(None, None, None)
            kb.barrier()
            with contextlib.ExitStack() as sc:
                ws = mk_ws(sc, 5, "C")
                xst = sb(sc, "xst2", [128, 2, D], F32)
                xstB = [kb.dbuf(), kb.dbuf()]
                h = sb(sc, "h2", [128, NCH, TS], BF16)
                hB = [Buf() for _ in range(NCH)]
                sq, sqB = h, hB
                rstd = sb(sc, "rstd2", [128, TS], F32)
                rstdB = Buf()
                tmpn = sb(sc, "tmpn2", [128, 2, TS], F32)
                tmpB = [Buf(), Buf()]
                zact = sb(sc, "zact2", [128, FCH, TS], BF16)
                zB = [Buf() for _ in range(FCH)]
                yn = sb(sc, "yn", [128, NCH, TS], F32)
                ynB = [Buf() for _ in range(NCH)]

                def norm_tile2(xsrc, xbufs, n, gs_col, sh_col, dst, dstB):
                    for c in range(NCH):
                        if c % 2 == 0:
                            kb.op(act, lambda: A_.activation(out=sq[:, c, 0:n], in_=xsrc(c), func=AF.Square), R=[xbufs[c]], W=[sqB[c]])
                        else:
                            kb.op(dve, lambda: V_.tensor_tensor(out=sq[:, c, 0:n], in0=xsrc(c), in1=xsrc(c), op=ALU.mult), R=[xbufs[c]], W=[sqB[c]])
                    bank, bB = nbank()
                    kb.mm([(bank[:, 0:n], onesb[:], sq[:, c, 0:n]) for c in range(NCH)], R=sqB + [cB], W=[bB])
                    kb.op(act, lambda: A_.activation(out=rstd[:, 0:n], in_=bank[:, 0:n], func=AF.Sqrt, bias=cst[:, 0:1]), R=[bB, cB], W=[rstdB])
                    kb.op(dve, lambda: V_.reciprocal(out=rstd[:, 0:n], in_=rstd[:, 0:n]), W=[rstdB])
                    for c in range(NCH):
                        tb = c % 2
                        if sh_col is None:
                            kb.op(dve, lambda: V_.scalar_tensor_tensor(out=dst(c), in0=xsrc(c), scalar=col(gs_col, c), in1=rstd[:, 0:n],
                                                                       op0=ALU.mult, op1=ALU.mult), R=[xbufs[c], rstdB, vecB], W=[dstB[c]])
                            continue
                        kb.op(dve, lambda: V_.scalar_tensor_tensor(out=tmpn[:, tb, 0:n], in0=xsrc(c), scalar=col(gs_col, c), in1=rstd[:, 0:n],
                                                                   op0=ALU.mult, op1=ALU.mult), R=[xbufs[c], rstdB, vecB], W=[tmpB[tb]])
                        if c % 2 == 0:
                            kb.op(act, lambda: A_.activation(out=dst(c), in_=tmpn[:, tb, 0:n], func=AF.Identity, bias=col(sh_col, c)),
                                  R=[tmpB[tb], vecB], W=[dstB[c]])
                        else:
                            kb.op(act, lambda: A_.activation(out=dst(c), in_=tmpn[:, tb, 0:n], func=AF.Identity, bias=col(sh_col, c)),
                                  R=[tmpB[tb], vecB], W=[dstB[c]])

                out_toks = []
                oi = 0
                for tt in range(NT):
                    ffn_tile_c = None
                    xs_ = lambda c, tt=tt: X[:, c, tt * TS:(tt + 1) * TS]
                    xb_ = [XB[c][tt] for c in range(NCH)]
                    l = 1
                    norm_tile2(xs_, xb_, TS, V_GS + l * 16 + 8, V_MOD[l] + 24, lambda c: h[:, c, :], hB)
                    units = []
                    groups = [(g * 512, 512) for g in range(5)] + [(2560, 256)]
                    for (c0, w) in groups:
                        units.append((ffn_wg[l, c0 // 512][:, 0:8 * w].rearrange("p (a b) -> p a b", a=8), 8, w))
                        units.append((ffn_wu[l, c0 // 512][:, 0:8 * w].rearrange("p (a b) -> p a b", a=8), 8, w))
                    for og in range(4):
                        units.append((ffn_wd[l, og, 0].rearrange("p (a b) -> p a b", a=11), 11, 256))
                        units.append((ffn_wd[l, og, 1].rearrange("p (a b) -> p a b", a=11), 11, 256))
                    ws.plan(units)
                    for (c0, w) in groups:
                        wgt, wgtB = ws.get(0)
                        wup, wupB = ws.get(1)
                        for m in range(w // 128):
                            fc = c0 // 128 + m
                            bg, bgB = nbank()
                            kb.mm([(bg[:], wgt[:, kc, m * 128:(m + 1) * 128], h[:, kc, :]) for kc in range(NCH)], R=hB + [wgtB], W=[bgB])
                            bu, buB = nbank()
                            kb.mm([(bu[:], wup[:, kc, m * 128:(m + 1) * 128], h[:, kc, :]) for kc in range(NCH)], R=hB + [wupB], W=[buB])
                            tb = fc % 2
                            kb.op(act, lambda: A_.activation(out=tmpn[:, tb, :], in_=bg[:], func=AF.Silu), R=[bgB], W=[tmpB[tb]])
                            kb.op(dve, lambda: V_.tensor_tensor(out=zact[:, fc, :], in0=tmpn[:, tb, :], in1=bu[:], op=ALU.mult), R=[tmpB[tb], buB], W=[zB[fc]])
                        ws.release()
                        ws.release()
                    for og in range(4):
                        wd, wdB = ws.get(0)
                        wd2, wd2B = ws.get(1)
                        for oo in range(2):
                            o = og * 2 + oo
                            bank, bB = nbank()
                            kb.mm([(bank[:], (wd if kc < 11 else wd2)[:, kc % 11, oo * 128:(oo + 1) * 128], zact[:, kc, :]) for kc in range(FCH)],
                                  R=zB + [wdB, wd2B], W=[bB])
                            xs = X[:, o, tt * TS:(tt + 1) * TS]
                            kb.op(dve, lambda: V_.scalar_tensor_tensor(out=xs, in0=bank[:], scalar=col(V_MOD[l] + 40, o), in1=xs, op0=ALU.mult, op1=ALU.add),
                                  R=[bB, vecB], W=[XB[o][tt]])
                        ws.release()
                        ws.release()
                    norm_tile2(xs_, xb_, TS, V_FN, None, lambda c: yn[:, c, :], ynB)
                    for sub in range(4):
                        s = oi % 2
                        oi += 1
                        for half in range(2):
                            bank, bB = nbank()
                            kb._pre(pe, ynB[half * 4:half * 4 + 4] + [identB], [bB])
                            ins = None
                            for j in range(4):
                                c = half * 4 + j
                                ins = P_.transpose(bank[:, j * 128:(j + 1) * 128], yn[:, c, sub * 128:(sub + 1) * 128], ident[:])
                            tok = pe.done(ins)
                            kb._post(tok, ynB[half * 4:half * 4 + 4] + [identB], [bB])
                            if half == 0:
                                kb.op(act, lambda: A_.activation(out=xst[:, s, 0:512], in_=bank[:], func=AF.Copy), R=[bB], W=[xstB[s]])
                            else:
                                kb.op(dve, lambda: V_.tensor_copy(out=xst[:, s, 512:1024], in_=bank[:]), R=[bB], W=[xstB[s]])
                        r0 = tt * TS + sub * 128
                        tk = kb.dma(sp, out_own[r0:r0 + 128, :], xst[:, s, :], R=[xstB[s]], sb=xstB[s])
                        out_toks.append(tk)
                for tk in out_toks:
                    sp.wait_tok(*tk)
                kb.barrier()
        else:
            kb.barrier()
        for e in (sp,):
            for b in xstB:
                e.wait_set(b.r)
                e.wait_set(b.w)
    return nc


def _t5_bucket(dist):
    dist = np.maximum(dist, 0)
    d = np.maximum(dist, 1).astype(np.float32)
    large = 16 + (np.log(d / 16) / np.float32(np.log(128 / 16)) * 16).astype(np.int32)
    large = np.minimum(large, 31)
    return np.where(dist < 16, dist, large)


def _tile_w(W, w):
    W = np.asarray(W, np.float32)
    K, N = W.shape
    return np.ascontiguousarray(W.reshape(K // 128, 128, N // w, w).transpose(2, 1, 0, 3).reshape(N // w, 128, (K // 128) * w))


def _tile_ffn(W):
    out = np.zeros((6, 128, 4096), np.float32)
    out[0:5] = _tile_w(W[:, 0:2560], 512)
    out[5, :, 0:2048] = _tile_w(W[:, 2560:2816], 256)[0]
    return out


def _fm(v, n):
    return np.ascontiguousarray(np.asarray(v, np.float32).reshape(n, 128).T)


_CACHE = {}


def _program(stop_after=99, debug=False):
    key = (stop_after, debug)
    if key not in _CACHE:
        _CACHE[key] = build_program(stop_after, debug)
    return _CACHE[key]


def make_in_maps(inp):
    f = lambda a: np.ascontiguousarray(np.asarray(a, dtype=np.float32))
    x = f(inp["x"]); c = f(inp["c"])
    rel_bias = f(inp["rel_bias"])
    shared = {
        "mod_w": np.stack([_tile_w(inp["mod_w"][l], 512) for l in range(2)]),
        "mod_b_T": np.ascontiguousarray(np.stack([_fm(inp["mod_b"][l], 48) for l in range(2)], axis=1)),
        "norm_mix_T": np.ascontiguousarray(np.stack([_fm(inp["norm_mix"][l], 8) for l in range(2)], axis=1)),
        "norm_ffn_T": np.ascontiguousarray(np.stack([_fm(inp["norm_ffn"][l], 8) for l in range(2)], axis=1)),
        "w_in": _tile_w(inp["lru_w_in"][0], 256),
        "conv_w_T": np.ascontiguousarray(np.stack([_fm(inp["lru_conv_w"][0][k], 10) for k in range(4)], axis=1)),
        "conv_b_T": _fm(inp["lru_conv_b"][0], 10),
        "w_gates": np.stack([_tile_w(inp["lru_w_gates"][0][hd], 512)[0] for hd in range(5)]),
        "b_gates_T": np.ascontiguousarray(np.stack([_fm(inp["lru_b_gates"][0][k], 10) for k in range(2)], axis=1)),
        "lambda_T": _fm(inp["lru_lambda"][0], 10),
        "w_out": _tile_w(inp["lru_w_out"][0], 256),
        "kv_mod_w": _tile_w(inp["kv_mod_w"], 512),
        "kv_mod_b_T": _fm(inp["kv_mod_b"], 16),
        "kv_norm_T": _fm(inp["kv_norm"], 8),
        "w_kv": _tile_w(inp["w_kv"], 512),
        "w_q": _tile_w(inp["attn_w_q"][0], 512),
        "w_o": _tile_w(inp["attn_w_o"][0], 512),
        "ffn_wg": np.stack([_tile_ffn(inp["ffn_w_gate"][l]) for l in range(2)]),
        "ffn_wu": np.stack([_tile_ffn(inp["ffn_w_up"][l]) for l in range(2)]),
        "ffn_wd": np.ascontiguousarray(np.stack([np.stack([_tile_w(inp["ffn_w_down"][l][hh * 1408:(hh + 1) * 1408], 256) for hh in range(2)], axis=1) for l in range(2)])),
        "final_norm_T": _fm(inp["final_norm"], 8),
        "bias31": np.ascontiguousarray(np.broadcast_to(rel_bias[:, 31][None, :], (128, 8))),
        "ident": np.eye(128, dtype=np.float32),
    }
    es = np.zeros((32, 32, 128), np.float32)
    for j in range(32):
        es[j, j, :] = 1.0
    shared["esel"] = es
    qi = np.arange(256)[None, :]
    ki = np.arange(256)[:, None]
    maps = []
    for core in range(8):
        b, k = divmod(core, 4)
        xb = x[b].reshape(32, 256, D)
        m = dict(shared)
        m["x_own"] = np.ascontiguousarray(xb[k::4].reshape(TOK, D))
        halo = np.zeros((8, 3, D), np.float32)
        hflag = np.ones((8, 3), np.float32)
        for i in range(8):
            gb = 4 * i + k
            if gb == 0:
                hflag[i] = 0.0
            else:
                halo[i] = xb[gb - 1][253:256]
        m["x_halo"] = np.ascontiguousarray(halo.reshape(24, NCH, 128).transpose(2, 1, 0))
        m["halo_flag"] = np.ascontiguousarray(np.broadcast_to(hflag.reshape(1, 24), (128, 24)))
        m["c_T"] = _fm(c[b], 8)
        sbt = np.empty((8, 5, 256, 256), np.float32)
        for s in range(5):
            dblk = (k - s) if s < 4 else (k + 1)
            dist = 256 * dblk + qi - ki
            idx = _t5_bucket(dist)
            g = rel_bias[:, idx]
            sbt[:, s] = np.where((dist >= 0)[None], g, np.float32(NEG))
        m["sbias"] = np.ascontiguousarray(sbt.reshape(8, 5, 2, 128, 256).transpose(0, 3, 1, 2, 4))
        vb = np.full((8, 32), -1e30, np.float32)
        of = np.zeros((8, 32), np.float32)
        for i in range(8):
            for r in range(4):
                for ip in range(8):
                    if ip < i or (ip == i and r < k):
                        vb[i, r * 8 + ip] = 0.0
            of[i, k * 8 + i] = 1.0
        m["vbias"] = np.ascontiguousarray(np.broadcast_to(vb[None], (128, 8, 32)))
        m["ownflag"] = np.ascontiguousarray(np.broadcast_to(of[None], (128, 8, 32)))
        rf = np.zeros((4,), np.float32); rf[k] = 1.0
        m["rankflag"] = np.ascontiguousarray(np.broadcast_to(rf[None], (128, 4)))
        maps.append(m)
    return maps


def kernel(**inputs):
    nc = _program()
    maps = make_in_maps(inputs)
    res = run_bass_kernel_spmd(nc, maps, core_ids=list(range(8)))
    out = np.empty((2, 32, 256, D), np.float32)
    for core in range(8):
        b, k = divmod(core, 4)
        out[b, k::4] = np.asarray(res.results[core]["out_own"], np.float32).reshape(8, 256, D)
    return out.reshape(2, 8192, D)
```
